# Optimizing a Trainium2 kernel written in Bass

```python
import math
import jax, jax.numpy as jnp
from jax import lax
import numpy as np

D_MODEL = 4096
BATCH = 4
SEQ = 2048
DEPTH = 1
DEC_BATCH = 128
DEC_SEQ = 1
PAST_LEN = 8192
PAGE_SIZE = 128

N_HEADS = 32
N_KV_HEADS = 8
HEAD_DIM = 64
GROUP = N_HEADS // N_KV_HEADS
ATTN_W = N_HEADS * HEAD_DIM
KV_W = N_KV_HEADS * HEAD_DIM
CONV_W = D_MODEL // 2
CONV_GROUPS = 16
CONV_K = 3
WINDOW = 128
Q_BLOCK = 128
N_BUCKETS = 32
MAX_DISTANCE = 128
D_FF = 11008
FFN_CONV_K = 3
EPS = 1e-5
NEG = -1e30
IN_W = 3 * CONV_W + ATTN_W + 2 * KV_W + 2 * D_MODEL

kernel_name = "hybrid_gated_conv_swa_convffn_step"


def rmsnorm(x, g):
    xf = x.astype(jnp.float32)
    r = lax.rsqrt(jnp.mean(xf * xf, axis=-1, keepdims=True) + EPS)
    return (xf * r).astype(x.dtype) * g


def t5_bucket(dist):
    max_exact = N_BUCKETS // 2
    d = jnp.maximum(dist, 0)
    df = jnp.maximum(d, 1).astype(jnp.float32)
    large = max_exact + (jnp.log(df / max_exact) / math.log(MAX_DISTANCE / max_exact)
                         * (N_BUCKETS - max_exact)).astype(jnp.int32)
    large = jnp.minimum(large, N_BUCKETS - 1)
    return jnp.where(d < max_exact, d, large)


def rel_bias_heads(rel_bias, dist):
    b = rel_bias[t5_bucket(dist)].astype(jnp.float32)
    b = jnp.transpose(b, (2, 0, 1))
    return b.reshape(N_KV_HEADS, GROUP, dist.shape[0], dist.shape[1])


def sink_softmax(s, sinks):
    sk = sinks.astype(jnp.float32).reshape(N_KV_HEADS, GROUP)[:, :, None, None]
    m = jnp.maximum(jnp.max(s, axis=-1, keepdims=True), sk)
    e = jnp.exp(s - m)
    denom = jnp.sum(e, axis=-1, keepdims=True) + jnp.exp(sk - m)
    return e / denom


def causal_dwconv(buf, u, w):
    k_w = w.shape[0]
    t = u.shape[1]
    ext = jnp.concatenate([buf, u], axis=1)
    y = w[0] * ext[:, 0:t]
    for j in range(1, k_w):
        y = y + w[j] * ext[:, j:j + t]
    return y, ext[:, ext.shape[1] - (k_w - 1):]


def swa_prompt(q, k, v, sinks, rel_bias):
    b, s_len = q.shape[0], q.shape[1]
    nb = s_len // Q_BLOCK
    qb = q.reshape(b, nb, Q_BLOCK, N_KV_HEADS, GROUP, HEAD_DIM)
    kb = k.reshape(b, nb, Q_BLOCK, N_KV_HEADS, HEAD_DIM)
    vb = v.reshape(b, nb, Q_BLOCK, N_KV_HEADS, HEAD_DIM)
    pad = ((0, 0), (1, 0), (0, 0), (0, 0), (0, 0))
    kband = jnp.concatenate([jnp.pad(kb, pad)[:, :-1], kb], axis=2)
    vband = jnp.concatenate([jnp.pad(vb, pad)[:, :-1], vb], axis=2)
    scale = HEAD_DIM ** -0.5
    sc = jnp.einsum('bnqkgd,bnjkd->bnkgqj', qb, kband,
                    preferred_element_type=jnp.float32) * scale
    qi = jnp.arange(Q_BLOCK)[:, None]
    kj = jnp.arange(2 * Q_BLOCK)[None, :]
    dist = qi + Q_BLOCK - kj
    sc = sc + rel_bias_heads(rel_bias, dist)
    kpos = (jnp.arange(nb)[:, None] - 1) * Q_BLOCK + jnp.arange(2 * Q_BLOCK)[None, :]
    valid = ((dist >= 0) & (dist <= WINDOW))[None] & (kpos >= 0)[:, None, :]
    sc = jnp.where(valid[None, :, None, None], sc, NEG)
    p = sink_softmax(sc, sinks).astype(v.dtype)
    o = jnp.einsum('bnkgqj,bnjkd->bnqkgd', p, vband).reshape(b, s_len, ATTN_W)
    lw = min(WINDOW, s_len)
    return o, k[:, s_len - lw:], v[:, s_len - lw:]


def swa_sample(q, k, v, k_buf, v_buf, sinks, rel_bias):
    db, t = q.shape[0], q.shape[1]
    lb = k_buf.shape[1]
    kall = jnp.concatenate([k_buf, k], axis=1)
    vall = jnp.concatenate([v_buf, v], axis=1)
    qpos = PAST_LEN + jnp.arange(t)
    kpos = PAST_LEN - lb + jnp.arange(lb + t)
    dist = qpos[:, None] - kpos[None, :]
    valid = (dist >= 0) & (dist <= WINDOW)
    qg = q.reshape(db, t, N_KV_HEADS, GROUP, HEAD_DIM)
    scale = HEAD_DIM ** -0.5
    sc = jnp.einsum('btkgd,bjkd->bkgtj', qg, kall,
                    preferred_element_type=jnp.float32) * scale
    sc = sc + rel_bias_heads(rel_bias, dist)
    sc = jnp.where(valid, sc, NEG)
    p = sink_softmax(sc, sinks).astype(v.dtype)
    o = jnp.einsum('bkgtj,bjkd->btkgd', p, vall).reshape(db, t, ATTN_W)
    return o, kall[:, t:], vall[:, t:]


def decoder_layer(x, conv_buf, ffn_buf, attend, attn_norm_g, w_in, conv_w, w_branch_a,
                  w_branch_b, w_out, ffn_norm_g, w_ffn_gate, w_ffn_up, ffn_conv_w,
                  ffn_conv_b, w_ffn_down):
    b, t, _ = x.shape
    h = rmsnorm(x, attn_norm_g)
    proj = h @ w_in
    splits = np.cumsum([CONV_W, CONV_W, CONV_W, ATTN_W, KV_W, KV_W, D_MODEL]).tolist()
    cb, cc, ch, q, k, v, ga, gb = jnp.split(proj, splits, axis=-1)
    z, conv_state = causal_dwconv(conv_buf, cc * ch, conv_w)
    branch_a = (cb * z) @ w_branch_a
    o, k_state, v_state = attend(q.reshape(b, t, N_HEADS, HEAD_DIM),
                                 k.reshape(b, t, N_KV_HEADS, HEAD_DIM),
                                 v.reshape(b, t, N_KV_HEADS, HEAD_DIM))
    branch_b = o @ w_branch_b
    merged = jax.nn.sigmoid(ga) * branch_a + jax.nn.sigmoid(gb) * branch_b
    x = x + merged @ w_out
    h2 = rmsnorm(x, ffn_norm_g)
    gc, ffn_state = causal_dwconv(ffn_buf, h2 @ w_ffn_gate, ffn_conv_w)
    f = jax.nn.silu(gc + ffn_conv_b) * (h2 @ w_ffn_up)
    x = x + f @ w_ffn_down
    return x, k_state, v_state, conv_state, ffn_state


def setup_inputs(seed: int = 0) -> dict:
    key = jax.random.key(seed)
    ks = jax.random.split(key, 24)
    f32 = jnp.float32
    win_buf = min(WINDOW, PAST_LEN)

    def nrm(k, shape, scale):
        return jax.random.normal(k, shape, f32) * scale

    return {
        "x_prompt": nrm(ks[0], (BATCH, SEQ, D_MODEL), 1.0),
        "x_sample": nrm(ks[1], (DEC_BATCH, DEC_SEQ, D_MODEL), 1.0),
        "state_k_window": nrm(ks[2], (DEPTH, DEC_BATCH, win_buf, N_KV_HEADS, HEAD_DIM), 1.0),
        "state_v_window": nrm(ks[3], (DEPTH, DEC_BATCH, win_buf, N_KV_HEADS, HEAD_DIM), 1.0),
        "state_conv": nrm(ks[4], (DEPTH, DEC_BATCH, CONV_K - 1, CONV_W), 1.0),
        "state_ffn_conv": nrm(ks[5], (DEPTH, DEC_BATCH, FFN_CONV_K - 1, D_FF), 1.0),
        "attn_norm_g": 1.0 + nrm(ks[6], (DEPTH, D_MODEL), 0.02),
        "w_in": nrm(ks[7], (DEPTH, D_MODEL, IN_W), D_MODEL ** -0.5),
        "conv_w": nrm(ks[8], (DEPTH, CONV_K, CONV_W), CONV_K ** -0.5),
        "w_branch_a": nrm(ks[9], (DEPTH, CONV_W, D_MODEL), CONV_W ** -0.5),
        "w_branch_b": nrm(ks[10], (DEPTH, ATTN_W, D_MODEL), ATTN_W ** -0.5),
        "sinks": nrm(ks[11], (DEPTH, N_HEADS), 1.0),
        "w_out": nrm(ks[12], (DEPTH, D_MODEL, D_MODEL), D_MODEL ** -0.5),
        "ffn_norm_g": 1.0 + nrm(ks[13], (DEPTH, D_MODEL), 0.02),
        "w_ffn_gate": nrm(ks[14], (DEPTH, D_MODEL, D_FF), D_MODEL ** -0.5),
        "w_ffn_up": nrm(ks[15], (DEPTH, D_MODEL, D_FF), D_MODEL ** -0.5),
        "ffn_conv_w": nrm(ks[16], (DEPTH, FFN_CONV_K, D_FF), FFN_CONV_K ** -0.5),
        "ffn_conv_b": nrm(ks[17], (DEPTH, D_FF), 0.01),
        "w_ffn_down": nrm(ks[18], (DEPTH, D_FF, D_MODEL), D_FF ** -0.5),
        "rel_bias": nrm(ks[19], (N_BUCKETS, N_HEADS), 0.5),
        "final_norm_g": 1.0 + nrm(ks[20], (D_MODEL,), 0.02),
    }


def reference(x_prompt, x_sample, state_k_window, state_v_window, state_conv, state_ffn_conv,
              attn_norm_g, w_in, conv_w, w_branch_a, w_branch_b, sinks, w_out, ffn_norm_g,
              w_ffn_gate, w_ffn_up, ffn_conv_w, ffn_conv_b, w_ffn_down, rel_bias,
              final_norm_g):
    xp, xs = x_prompt, x_sample
    pk, pv, pc, pf = [], [], [], []
    sk, sv, sc, sf = [], [], [], []
    for l in range(DEPTH):
        weights = (attn_norm_g[l], w_in[l], conv_w[l], w_branch_a[l], w_branch_b[l], w_out[l],
                   ffn_norm_g[l], w_ffn_gate[l], w_ffn_up[l], ffn_conv_w[l], ffn_conv_b[l],
                   w_ffn_down[l])
        sinks_l = sinks[l]
        zc = jnp.zeros((xp.shape[0], CONV_K - 1, CONV_W), xp.dtype)
        zf = jnp.zeros((xp.shape[0], FFN_CONV_K - 1, D_FF), xp.dtype)
        attend_p = lambda q, k, v, s_=sinks_l: swa_prompt(q, k, v, s_, rel_bias)
        xp, k1, v1, c1, f1 = decoder_layer(xp, zc, zf, attend_p, *weights)
        pk.append(k1); pv.append(v1); pc.append(c1); pf.append(f1)
        kb_l, vb_l = state_k_window[l], state_v_window[l]
        attend_s = lambda q, k, v, s_=sinks_l, kb=kb_l, vb=vb_l: swa_sample(q, k, v, kb, vb, s_, rel_bias)
        xs, k2, v2, c2, f2 = decoder_layer(xs, state_conv[l], state_ffn_conv[l], attend_s, *weights)
        sk.append(k2); sv.append(v2); sc.append(c2); sf.append(f2)
    y_prompt = rmsnorm(xp, final_norm_g)
    y_sample = rmsnorm(xs, final_norm_g)
    return (y_prompt, y_sample, jnp.stack(pk), jnp.stack(pv), jnp.stack(pc), jnp.stack(pf),
            jnp.stack(sk), jnp.stack(sv), jnp.stack(sc), jnp.stack(sf))
```

```python
import bisect
import math
from contextlib import ExitStack

import numpy as np
import concourse.bass as bass
import concourse.mybir as mybir
from concourse.bass_utils import run_bass_kernel_spmd

F32 = mybir.dt.float32
BF16 = mybir.dt.bfloat16
AF = mybir.ActivationFunctionType
ALU = mybir.AluOpType
AX = mybir.AxisListType

NCORES = 8
BATCH = 4
SEQ = 2048
DEC_BATCH = 128
WINDOW = 128
HD = 64
N_BUCKETS = 32
MAX_DISTANCE = 128
EPS = 1e-5
NEG = -1e30
NGRP = 2
GP = 512
GS = 8
NHALO = 32
V0 = 28
NM = NHALO + GP + GS
NH = 256
TT = [(V0, V0 + (NM - V0) // 2), (V0 + (NM - V0) // 2, NM)]
HT = [(0, NH)]


class Cfg:
    def __init__(self, D=4096, H=32, KV=8, FF=11008, partmax=22):
        self.D, self.H, self.KV, self.FF = D, H, KV, FF
        self.G = H // KV
        self.AW = H * HD
        self.KW = KV * HD
        self.CW = D // 2
        self.DC, self.AC, self.CC, self.FC = D // 128, self.AW // 128, self.CW // 128, FF // 128
        self.INW = 3 * self.CW + self.AW + 2 * self.KW + 2 * D
        self.oCB, self.oCC, self.oCH = 0, self.CW, 2 * self.CW
        self.oQ = 3 * self.CW
        self.oK = self.oQ + self.AW
        self.oV = self.oK + self.KW
        self.oGA = self.oV + self.KW
        self.oGB = self.oGA + D
        self.KMAX = max(self.DC, self.AC, self.CC)
        np_ = -(-self.FC // partmax)
        base, rem = divmod(self.FC, np_)
        self.parts = []
        s = 0
        for i in range(np_):
            n = base + (1 if i < rem else 0)
            self.parts.append((s, n))
            s += n
        self.PMAX = max(n for _, n in self.parts)
        assert self.PMAX <= self.KMAX
        assert self.AC == self.CC == self.DC // 2


REAL = Cfg()


class Tok:
    __slots__ = ("name", "w", "r", "dsem", "dcnt")

    def __init__(self, name):
        self.name = name
        self.w = None
        self.r = []
        self.dsem = None
        self.dcnt = 0


class Sched:
    ENG = ["pe", "act", "dve", "pool", "sp"]

    def __init__(self, nc, stack):
        self.nc = nc
        self.stack = stack
        self.ops = {e: [] for e in self.ENG}
        self.sigseq = {e: [] for e in self.ENG}
        self.sem = {e: stack.enter_context(nc.semaphore("s_" + e)) for e in self.ENG}
        self.waited = {}
        self.store_evs = []
        self.dma_toks = []

    def tok(self, name):
        return Tok(name)

    def toks(self, name, n):
        return [Tok(f"{name}{i}") for i in range(n)]

    def _resolve(self, ev):
        if ev[0] == "d":
            return ev[1], ev[2]
        _, eng, seq = ev
        ss = self.sigseq[eng]
        i = bisect.bisect_left(ss, seq)
        if i == len(ss):
            last = len(self.ops[eng]) - 1
            rec = self.ops[eng][last]
            assert not rec[2] and not rec[3], (eng, seq, last)
            rec[2] = True
            ss.append(last)
            i = len(ss) - 1
        return self.sem[eng], i + 1

    def _waits(self, eng, reads, writes):
        deps = []
        for t in reads:
            if t.w is not None:
                deps.append(t.w)
        for t in writes:
            if t.w is not None:
                deps.append(t.w)
            deps.extend(t.r)
        need = {}
        for ev in deps:
            if ev[0] == "c" and ev[1] == eng and eng == "pe":
                continue
            sem, val = self._resolve(ev)
            k = id(sem)
            if k not in need or need[k][1] < val:
                need[k] = (sem, val)
        out = []
        for k, (sem, val) in need.items():
            if self.waited.get((eng, k), 0) >= val:
                continue
            self.waited[(eng, k)] = val
            out.append((sem, val))
        return out

    def op(self, eng, fn, reads=(), writes=(), signal=None):
        if signal is None:
            signal = eng != "pe"
        waits = self._waits(eng, reads, writes)
        seq = len(self.ops[eng])
        self.ops[eng].append([waits, fn, signal, False, None])
        if signal:
            self.sigseq[eng].append(seq)
        ev = ("c", eng, seq)
        for t in reads:
            t.r.append(ev)
        for t in writes:
            t.w = ev
            t.r = []
        return ev

    def dma(self, q, out, in_, reads=(), writes=(), semtok=None, store=False, **kw):
        if semtok is None:
            semtok = writes[0] if writes else reads[0]
        if semtok.dsem is None:
            semtok.dsem = self.stack.enter_context(self.nc.semaphore("d_" + semtok.name))
            self.dma_toks.append(semtok)
        waits = self._waits(q, reads, writes)
        semtok.dcnt += 16
        ev = ("d", semtok.dsem, semtok.dcnt)

        def fn(e, out=out, in_=in_, kw=kw):
            return e.dma_start(out=out, in_=in_, **kw)

        self.ops[q].append([waits, fn, False, True, semtok.dsem])
        for t in reads:
            t.r.append(ev)
        for t in writes:
            t.w = ev
            t.r = []
        if store:
            self.store_evs.append(ev)
        return ev

    def barrier(self, carriers):
        arr = {}
        for eng in self.ENG:
            t = Tok("bar_" + eng)
            self.op(eng, carriers[eng], writes=[t], signal=True)
            arr[eng] = t
        alltoks = list(arr.values())
        for eng in self.ENG:
            waits = self._waits(eng, alltoks, [])
            for dt_ in self.dma_toks:
                k = id(dt_.dsem)
                if self.waited.get((eng, k), 0) < dt_.dcnt:
                    self.waited[(eng, k)] = dt_.dcnt
                    waits.append((dt_.dsem, dt_.dcnt))
            t = Tok("barw_" + eng)
            self.op(eng, carriers[eng], writes=[t], signal=True)
            self.ops[eng][-1][0] = waits + self.ops[eng][-1][0]

    def run_engine(self, eng, e):
        sem = self.sem[eng]
        for waits, fn, signal, is_dma, dsem in self.ops[eng]:
            for s, v in waits:
                e.wait_ge(s, v)
            ins = fn(e)
            if is_dma:
                ins.then_inc(dsem, 16)
            elif signal:
                ins.then_inc(sem, 1)
        if eng == "sp":
            need = {}
            for ev in self.store_evs:
                k = id(ev[1])
                if k not in need or need[k][1] < ev[2]:
                    need[k] = (ev[1], ev[2])
            for s, v in need.values():
                e.wait_ge(s, v)

    def emit(self):
        nc = self.nc
        with nc.Block() as block:
            if self.ops["pe"]:
                block.tensor(lambda e: self.run_engine("pe", e))
            if self.ops["act"]:
                block.scalar(lambda e: self.run_engine("act", e))
            if self.ops["dve"]:
                block.vector(lambda e: self.run_engine("dve", e))
            if self.ops["pool"]:
                block.gpsimd(lambda e: self.run_engine("pool", e))
            block.sync(lambda e: self.run_engine("sp", e))


def bc(ap, shape):
    return ap.broadcast_to(list(shape))


def build(cfg, dbg=False, maxsteps=None):
    D, H, KV, FF, G = cfg.D, cfg.H, cfg.KV, cfg.FF, cfg.G
    DC, AC, CC, FC = cfg.DC, cfg.AC, cfg.CC, cfg.FC
    KMAX = cfg.KMAX
    NX = NH + NM
    scale = HD ** -0.5

    nc = bass.Bass("TRN2", target_bir_lowering=False)

    def din(name, shape):
        return nc.dram_tensor(name, list(shape), F32, kind="ExternalInput").ap()

    def dout(name, shape):
        return nc.dram_tensor(name, list(shape), F32, kind="ExternalOutput").ap()

    xT = din("xT", [NGRP, D, NX])
    w_in = din("w_in", [D, cfg.INW])
    w_a = din("w_a", [cfg.CW, D])
    w_b = din("w_b", [cfg.AW, D])
    w_o = din("w_o", [D, D])
    w_g = din("w_g", [D, FF])
    w_u = din("w_u", [D, FF])
    w_d = din("w_d", [FF, D])
    g1T = din("g1T", [128, DC])
    g2T = din("g2T", [128, DC])
    gfT = din("gfT", [128, DC])
    cwT = din("cwT", [128, CC, 3])
    fwT = din("fwT", [128, FC, 3])
    fbT = din("fbT", [128, FC])
    biasT = din("biasT", [128, H, 2, 128])
    fmask = din("fmask", [128, 1])
    biasS = din("biasS", [128, H])
    biasSn = din("biasSn", [128, H])
    sinkb = din("sinkb", [128, H])
    scT = din("scT", [NGRP, 128, CC, 2, GS])
    sfT = din("sfT", [NGRP, 128, FC, 2, GS])
    kstT = din("kstT", [NGRP, GS, 128, KV, 128])
    vstd = din("vstd", [NGRP, GS, 128, KV, 128])
    st_k = din("st_k", [NGRP * GS, 128, cfg.KW])
    st_v = din("st_v", [NGRP * GS, 128, cfg.KW])
    st_c = din("st_c", [NGRP * GS, 2, cfg.CW])
    st_f = din("st_f", [NGRP * GS, 2, FF])

    yT = dout("yT", [NGRP, D, GP + GS])
    kp_o = dout("kp_o", [NGRP, 128, cfg.KW])
    vp_o = dout("vp_o", [NGRP, 128, cfg.KW])
    ks_new = dout("ks_new", [NGRP, GS, cfg.KW])
    vs_new = dout("vs_new", [NGRP, GS, cfg.KW])
    ucT_o = dout("ucT_o", [NGRP, 128, CC, 2 + GS])
    gcT_o = dout("gcT_o", [NGRP, 128, FC, 2 + GS])
    ks_o = dout("ks_o", [NGRP * GS, 127, cfg.KW])
    vs_o = dout("vs_o", [NGRP * GS, 127, cfg.KW])
    c0_o = dout("c0_o", [NGRP * GS, cfg.CW])
    f0_o = dout("f0_o", [NGRP * GS, FF])
    dbg_o = {}
    if dbg:
        dbg_o["hT"] = dout("dbg_hT", [NGRP, 128, DC, NM])
        dbg_o["aT"] = dout("dbg_aT", [NGRP, 128, CC, NM])
        dbg_o["oT"] = dout("dbg_oT", [NGRP, 128, AC, NM])
        dbg_o["mT"] = dout("dbg_mT", [NGRP, 128, DC, NM])
        dbg_o["xm"] = dout("dbg_xm", [NGRP, 128, DC, NM])

    with ExitStack() as st:
        S = Sched(nc, st)

        def sb(name, shape, dt=F32):
            return st.enter_context(nc.sbuf_tensor(name, list(shape), dt))

        def ps(name, shape, dt=F32):
            return st.enter_context(nc.psum_tensor(name, list(shape), dt))

        WA = DC * NM
        arenaA = sb("arenaA", [128, WA])
        xmid = arenaA[:, :].rearrange("p (c n) -> p c n", c=DC)
        o1 = DC * NM // 2
        o2 = o1 + DC * (NM // 2) // 2
        hmain = arenaA[:, 0:o1].bitcast(BF16).rearrange("p (c n) -> p c n", c=DC)
        hhalo = arenaA[:, o1:o1 + DC * NH // 2].bitcast(BF16).rearrange("p (c n) -> p c n", c=DC)
        aT = arenaA[:, o1:o1 + CC * NM // 2].bitcast(BF16).rearrange("p (c n) -> p c n", c=CC)
        QT = arenaA[:, o2:o2 + AC * NM // 2].bitcast(BF16).rearrange("p (c n) -> p c n", c=AC)
        assert o2 + AC * NM // 2 <= WA and o1 + CC * NM // 2 <= o2 and o1 + DC * NH // 2 <= o2

        wK = KV * NX // 2
        wV = 6 * KV * 128 // 2
        wX = max(H * 256, DC * NM // 2)
        wB = max(wK + wV + wX, DC * NM // 2 + cfg.PMAX * NM // 2)
        arenaB = sb("arenaB", [128, wB])
        Kd = arenaB[:, 0:wK].bitcast(BF16).rearrange("p (k n) -> p k n", k=KV)
        Vd = arenaB[:, wK:wK + wV].bitcast(BF16).rearrange("p (b k d) -> p b k d", b=6, k=KV)
        bT = arenaB[:, wK + wV:wK + wV + H * 256].rearrange("p (h v i) -> p h v i", h=H, v=2)
        mT = arenaB[:, wK + wV:wK + wV + DC * NM // 2].bitcast(BF16).rearrange("p (c n) -> p c n", c=DC)
        h2T = arenaB[:, 0:DC * NM // 2].bitcast(BF16).rearrange("p (c n) -> p c n", c=DC)
        fT = arenaB[:, DC * NM // 2:DC * NM // 2 + cfg.PMAX * NM // 2].bitcast(BF16).rearrange("p (c n) -> p c n", c=cfg.PMAX)

        hmk = S.toks("hm", DC)
        hhk = S.toks("hh", DC)
        aTk = S.toks("aT", CC)
        QTk = S.toks("QT", AC)
        Kdk = S.toks("Kd", KV)
        Vdk = S.toks("Vd", 6)
        bTk = S.tok("bT")
        mTk = S.toks("mT", DC)
        xmk = S.toks("xm", DC)
        h2k = S.toks("h2", DC)
        fTk = S.toks("fT", cfg.PMAX)

        NWB = 4
        wbf = [sb(f"wbf{i}", [128, KMAX, 128], BF16) for i in range(NWB)]
        wbk = S.toks("wbf", NWB)

        def const_load(name, src, shape, dt=F32):
            t = sb(name, shape, dt)
            k = S.tok(name)
            S.dma("sp", t[:], src, writes=[k])
            return t, k

        g1, g1k = const_load("g1", g1T, [128, DC])
        g2, g2k = const_load("g2", g2T, [128, DC])
        gf, gfk = const_load("gf", gfT, [128, DC])
        cw, cwk = const_load("cw", cwT, [128, CC, 3])
        fw, fwk = const_load("fw", fwT, [128, FC, 3])
        fb, fbk = const_load("fb", fbT, [128, FC])
        fm, fmk = const_load("fm", fmask, [128, 1])
        bS, bSk = const_load("bS", biasS, [128, H])
        bSn, bSnk = const_load("bSn", biasSn, [128, H])
        esk, eskk = const_load("esk", sinkb, [128, H])
        S.op("act", lambda e: e.activation(out=esk[:], in_=esk[:], func=AF.Exp), reads=[eskk], writes=[eskk])

        ones_bf = sb("ones_bf", [128, 128], BF16)
        onesk = S.tok("ones")
        S.op("dve", lambda e: e.memset(ones_bf[:], 1.0), writes=[onesk])
        ones_f = sb("ones_f", [1, 128])
        S.op("dve", lambda e: e.memset(ones_f[:], 1.0), writes=[onesk])
        scr = {e_: sb("scr_" + e_, [128, 8]) for e_ in ["act", "dve", "pool"]}
        for e_ in ["act", "dve", "pool"]:
            S.op("dve", lambda e, e_=e_: e.memset(scr[e_][:], 0.0), writes=[S.tok("scr" + e_)])
        scr_sp_src = sb("scr_sp_src", [1, 16])
        scr_sp_dst = sb("scr_sp_dst", [1, 16])
        S.op("pool", lambda e: e.memset(scr_sp_src[:], 0.0), writes=[S.tok("x")])

        PS = [ps(f"ps{i}", [128, 512]) for i in range(8)]
        PSk = S.toks("ps", 8)

        spk = S.tok("spbar")
        carriers = {
            "pe": lambda e: e.matmul(PS[7][0:1, 0:1], lhsT=ones_f[0:1, 0:1], rhs=ones_f[0:1, 0:1], start=True, stop=True),
            "act": lambda e: e.activation(out=scr["act"][:, 0:1], in_=scr["act"][:, 1:2], func=AF.Copy),
            "dve": lambda e: e.memset(scr["dve"][:, 0:1], 0.0),
            "pool": lambda e: e.memset(scr["pool"][:, 0:1], 0.0),
            "sp": None,
        }

        def barrier(drain=True):
            arr = {}
            for eng in ["pe", "act", "dve", "pool"]:
                t = Tok("bar_" + eng)
                rw = [PSk[7]] if eng == "pe" else []
                S.op(eng, carriers[eng], writes=[t] + rw, signal=True)
                arr[eng] = t
            alltoks = list(arr.values())
            for eng in ["pe", "act", "dve", "pool"]:
                rw = [PSk[7]] if eng == "pe" else []
                S.op(eng, carriers[eng], reads=alltoks, writes=rw, signal=True)
                extra = []
                for dt_ in (S.dma_toks if drain else []):
                    k = id(dt_.dsem)
                    if S.waited.get((eng, k), 0) < dt_.dcnt:
                        S.waited[(eng, k)] = dt_.dcnt
                        extra.append((dt_.dsem, dt_.dcnt))
                S.ops[eng][-1][0] = S.ops[eng][-1][0] + extra
            S.dma("sp", scr_sp_dst[:], scr_sp_src[:], reads=alltoks, writes=[spk])
            extra = []
            for dt_ in (S.dma_toks if drain else []):
                k = id(dt_.dsem)
                if S.waited.get(("sp", k), 0) < dt_.dcnt and dt_ is not spk:
                    S.waited[("sp", k)] = dt_.dcnt
                    extra.append((dt_.dsem, dt_.dcnt))
            S.ops["sp"][-1][0] = S.ops["sp"][-1][0] + extra

        steps = []

        def wstep(wap, kc, ncol, fn, dup=False):
            steps.append(dict(w=wap, kc=kc, ncol=ncol, dup=dup, fn=fn))

        def cstep(fn):
            steps.append(dict(w=None, fn=fn))

        def wview(W, r0, kc, c0, ncol):
            return W[r0:r0 + kc * 128, c0:c0 + ncol].rearrange("(c p) n -> p c n", p=128)

        cast_rr = [0]

        def run_steps():
            wsteps = [s_ for s_ in steps if s_["w"] is not None]
            for i, s_ in enumerate(wsteps):
                s_["wi"] = i
            issued_dma = [0]

            def do_dma(upto):
                while issued_dma[0] < min(upto, len(wsteps)):
                    j = issued_dma[0]
                    s_ = wsteps[j]
                    wl = j % NWB
                    kc, ncol = s_["kc"], s_["ncol"]
                    for r in range(2 if s_["dup"] else 1):
                        S.dma("pool", wbf[wl][:, 0:kc, r * ncol:(r + 1) * ncol], s_["w"], writes=[wbk[wl]])
                    issued_dma[0] += 1

            for s_ in steps:
                if s_["w"] is None:
                    s_["fn"]()
                    continue
                i = s_["wi"]
                do_dma(i + NWB)
                s_["fn"](wbf[i % NWB], wbk[i % NWB])
            steps.clear()

        mm_rr = [0]

        def mm_main(wb_, wbk_, kc, rhs_fn, rhs_toks, evac, tiles=TT, M=128):
            base = (mm_rr[0] % 2) * 3
            mm_rr[0] += 1
            for ti, (c0, c1) in enumerate(tiles):
                p_, pk_ = PS[base + ti], PSk[base + ti]
                for k in range(kc):
                    S.op("pe", lambda e, p_=p_, k=k, c0=c0, c1=c1: e.matmul(
                        p_[0:M, 0:c1 - c0], lhsT=wb_[:, k, 0:M], rhs=rhs_fn(k, c0, c1), start=(k == 0), stop=(k == kc - 1)),
                        reads=[wbk_] + rhs_toks(k), writes=[pk_])
                evac(p_[0:M, 0:c1 - c0], ti, c0, c1, pk_)

        NTMP = 3
        tmpf = [sb(f"tmpf{i}", [128, NM + 4]) for i in range(NTMP)]
        tmpk = S.toks("tmpf", NTMP)
        tmp_rr = [0]

        def gettmp():
            i = tmp_rr[0] % NTMP
            tmp_rr[0] += 1
            return tmpf[i], tmpk[i]

        NXS = 4
        xs = [sb(f"xs{i}", [128, NX]) for i in range(NXS)]
        xsk = S.toks("xs", NXS)
        sqb = [sb(f"sq{i}", [128, NX], BF16) for i in range(2)]
        sqk = S.toks("sq", 2)
        rstd = sb("rstd", [128, NX])
        rstdk = S.tok("rstd")
        S.op("pool", lambda e: e.memset(rstd[:], 1.0), writes=[rstdk])
        ucs = sb("ucs", [128, CC, 2 + GS])
        ucsk = S.tok("ucs")
        gcr = [sb(f"gcr{i}", [128, 2 + GS]) for i in range(2)]
        gcrk = S.toks("gcr", 2)
        scs = sb("scs", [128, CC, 2, GS])
        scsk = S.tok("scs")
        sfr = [sb(f"sfr{i}", [128, 2, GS]) for i in range(2)]
        sfrk = S.toks("sfr", 2)
        ysl = [t_[:, 0:NM] for t_ in tmpf]
        yslk = tmpk
        sT = [x_[:, 0:512] for x_ in xs[0:2]]
        sTk = xsk[0:2]
        pTb = [None, None]
        pTs = [[q_[:, 0:512] for q_ in sqb], pTb]
        pTks = [sqk, [None, None]]
        dn = rstd[:, 0:512]
        dnk = rstdk
        kst16 = [sb(f"kst16_{i}", [128, KV, 128], BF16) for i in range(2)]
        kst16k = S.toks("kst16", 2)
        vst16 = [sb(f"vst16_{i}", [128, KV, 128], BF16) for i in range(2)]
        vst16k = S.toks("vst16", 2)
        vnT = sb("vnT", [128, KV, GS])
        vnTk = S.tok("vnT")
        pS = sb("pS", [128, GS * H], BF16)
        pSk = S.tok("pS")
        pnb, pnbk = tmpf[2][:, 0:GS * H], tmpk[2]
        prd = sb("prd", [128, AC, GS])
        prdk = S.tok("prd")
        assert cfg.KW // 128 <= AC
        vst, vstk = prd[:, 0:cfg.KW // 128, :], prdk
        selh = sb("selh", [128, 2, 128])
        selk = S.tok("selh")
        S.op("pool", lambda e: e.memset(selh[:], 0.0), writes=[selk])
        S.op("pool", lambda e: e.memset(selh[0:64, 0, :], 1.0), writes=[selk])
        S.op("pool", lambda e: e.memset(selh[64:128, 1, :], 1.0), writes=[selk])
        assert GS * H <= NM and NTMP >= 3
        sS, sSk = tmpf[0][:, 0:GS * H], tmpk[0]
        osm, osmk = tmpf[1][:, 0:GS * H], tmpk[1]
        ccb = sb("ccb", [128, NM])
        ccbk = S.tok("ccb")
        ub = sb("ub", [128, NM])
        ubk = S.tok("ub")
        zb = sb("zb", [128, NM])
        zbk = S.tok("zb")
        pTb[0], pTb[1] = ccb[:, 0:256].bitcast(BF16), ub[:, 0:256].bitcast(BF16)
        kvtm, kvtmk = zb[:, 0:512], zbk
        pTks[1][0], pTks[1][1] = ccbk, ubk
        S.op("pool", lambda e: e.memset(zb[:], 0.0), writes=[zbk])

        d2dk = S.tok("d2d")
        S.dma("sp", ks_o[:, :, :], st_k[:, 1:128, :], writes=[d2dk], store=True)
        S.dma("sp", vs_o[:, :, :], st_v[:, 1:128, :], writes=[d2dk], store=True)
        S.dma("sp", c0_o[:, :], st_c[:, 1, :], writes=[d2dk], store=True)
        S.dma("sp", f0_o[:, :], st_f[:, 1, :], writes=[d2dk], store=True)

        def ssq_acc(src_ap, src_tok, c, nchunks, tiles, col0, banks):
            i = c % 2
            ncols = src_ap.shape[-1]
            S.op("act", lambda e: e.activation(out=sqb[i][:, col0:col0 + ncols], in_=src_ap, func=AF.Square),
                 reads=[src_tok], writes=[sqk[i]])
            for bi, (c0, c1) in enumerate(tiles):
                S.op("pe", lambda e, bi=bi, c0=c0, c1=c1: e.matmul(PS[banks[bi]][:, 0:c1 - c0], lhsT=ones_bf[:, :], rhs=sqb[i][:, c0:c1],
                                                                   start=(c == 0), stop=(c == nchunks - 1)),
                     reads=[sqk[i], onesk], writes=[PSk[banks[bi]]])

        def rstd_fin(tiles, banks):
            for bi, (c0, c1) in enumerate(tiles):
                S.op("act", lambda e, bi=bi, c0=c0, c1=c1: e.activation(out=rstd[:, c0:c1], in_=PS[banks[bi]][:, 0:c1 - c0], func=AF.Sqrt,
                                                                       scale=1.0 / D, bias=EPS),
                     reads=[PSk[banks[bi]]], writes=[rstdk])
            n = tiles[-1][1]
            S.op("dve", lambda e: e.reciprocal(out=rstd[:, 0:n], in_=rstd[:, 0:n]), reads=[rstdk], writes=[rstdk])

        XT3 = [(0, 260), (260, 520), (520, NX)]

        for g in range(NGRP):
            def p0(g=g):
                for c in range(DC):
                    i = c % NXS
                    S.dma("sp", xs[i][:, :], xT[g, c * 128:(c + 1) * 128, :], writes=[xsk[i]])
                    ssq_acc(xs[i][:, :], xsk[i], c, DC, XT3, 0, [0, 1, 2])
                rstd_fin(XT3, [0, 1, 2])
                for c in range(DC):
                    i = c % NXS
                    S.dma("sp", xs[i][:, :], xT[g, c * 128:(c + 1) * 128, :], writes=[xsk[i]])
                    S.op("dve", lambda e, c=c, i=i: e.scalar_tensor_tensor(out=hhalo[:, c, :], in0=xs[i][:, 0:NH], scalar=g1[:, c:c + 1],
                                                                           in1=rstd[:, 0:NH], op0=ALU.mult, op1=ALU.mult),
                         reads=[xsk[i], g1k, rstdk], writes=[hhk[c]])
                    S.op("dve", lambda e, c=c, i=i: e.scalar_tensor_tensor(out=hmain[:, c, :], in0=xs[i][:, NH:NX], scalar=g1[:, c:c + 1],
                                                                           in1=rstd[:, NH:NX], op0=ALU.mult, op1=ALU.mult),
                         reads=[xsk[i], g1k, rstdk], writes=[hmk[c]])
                S.dma("sp", scs[:], scT[g], writes=[scsk])
                S.dma("sp", bT[:], biasT, writes=[bTk])
            cstep(p0)

            for k in range(KV):
                def fK(wb_, wbk_, k=k):
                    def evac(p_, ti, c0, c1, pk_, k=k):
                        S.op("act", lambda e: e.activation(out=Kd[:, k, NH + c0:NH + c1], in_=p_, func=AF.Copy), reads=[pk_], writes=[Kdk[k]])
                    mm_main(wb_, wbk_, DC, lambda kk, c0, c1: hmain[:, kk, c0:c1], lambda kk: [hmk[kk]], evac)

                    def evach(p_, ti, c0, c1, pk_, k=k):
                        S.op("act", lambda e: e.activation(out=Kd[:, k, c0:c1], in_=p_, func=AF.Copy), reads=[pk_], writes=[Kdk[k]])
                    mm_main(wb_, wbk_, DC, lambda kk, c0, c1: hhalo[:, kk, c0:c1], lambda kk: [hhk[kk]], evach, tiles=HT)
                wstep(wview(w_in, 0, DC, cfg.oK + k * HD, HD), DC, HD, fK, dup=True)
            def blk_rhs(b):
                if b < 2:
                    return (lambda kk, b=b: hhalo[:, kk, b * 128:(b + 1) * 128]), (lambda kk: [hhk[kk]]), 128
                if b < 6:
                    return (lambda kk, b=b: hmain[:, kk, NHALO + (b - 2) * 128:NHALO + (b - 1) * 128]), (lambda kk: [hmk[kk]]), 128
                return (lambda kk: hmain[:, kk, NM - GS:NM]), (lambda kk: [hmk[kk]]), GS

            for nb in range(cfg.KW // 128):
                def fV(wb_, wbk_, nb=nb, g=g):
                    bank = (mm_rr[0] % 2) * 3
                    mm_rr[0] += 1
                    p_, pk_ = PS[bank], PSk[bank]
                    for kk in range(DC):
                        S.op("pe", lambda e, kk=kk, p_=p_: e.matmul(p_[:, 0:GS], lhsT=wb_[:, kk, 0:128], rhs=hmain[:, kk, NM - GS:NM],
                                                                     start=(kk == 0), stop=(kk == DC - 1)),
                             reads=[wbk_, hmk[kk]], writes=[pk_])
                    S.op("act", lambda e, p_=p_: e.activation(out=vst[:, nb, :], in_=p_[:, 0:GS], func=AF.Copy), reads=[pk_], writes=[vstk])
                    if nb == cfg.KW // 128 - 1:
                        for r in range(2):
                            for half in range(2):
                                S.dma("sp", vnT[half * 64:(half + 1) * 64, r:KV:2, :], vst[r * 64:(r + 1) * 64, :, :], reads=[vstk], writes=[vnTk])
                    for b in range(7):
                        lf, tf, M = blk_rhs(b)
                        bank = (mm_rr[0] % 2) * 3
                        mm_rr[0] += 1
                        p_, pk_ = PS[bank], PSk[bank]
                        for kk in range(DC):
                            S.op("pe", lambda e, kk=kk, lf=lf, M=M, p_=p_: e.matmul(p_[0:M, 0:128], lhsT=lf(kk), rhs=wb_[:, kk, 0:128],
                                                                                 start=(kk == 0), stop=(kk == DC - 1)),
                                 reads=[wbk_] + tf(kk), writes=[pk_])
                        if b < 6:
                            src = p_[:, 0:128].rearrange("p (k d) -> p k d", k=2)
                            for r in range(2):
                                S.op("act", lambda e, b=b, r=r, src=src: e.activation(out=Vd[:, b, 2 * nb:2 * nb + 2, r * 64:(r + 1) * 64], in_=src, func=AF.Copy),
                                     reads=[pk_], writes=[Vdk[b]])
                        if b == 5:
                            S.op("act", lambda e, p_=p_: e.activation(out=kvtm[:, nb * 128:(nb + 1) * 128], in_=p_[:, 0:128], func=AF.Copy), reads=[pk_], writes=[kvtmk])
                            if nb == cfg.KW // 128 - 1:
                                S.dma("sp", vp_o[g, :, :], kvtm[:, 0:cfg.KW], reads=[kvtmk], store=True)
                        if b == 6:
                            t_, tk_ = gettmp()
                            S.op("dve", lambda e, p_=p_, t_=t_: e.tensor_copy(out=t_[0:GS, 0:128], in_=p_[0:GS, 0:128]), reads=[pk_], writes=[tk_])
                            S.dma("sp", vs_new[g, :, nb * 128:(nb + 1) * 128], t_[0:GS, 0:128], reads=[tk_], store=True)
                wstep(wview(w_in, 0, DC, cfg.oV + nb * 128, 128), DC, 128, fV)
            for nb in range(cfg.KW // 128):
                def fKt(wb_, wbk_, nb=nb, g=g):
                    for b in (5, 6):
                        lf, tf, M = blk_rhs(b)
                        bank = (mm_rr[0] % 2) * 3
                        mm_rr[0] += 1
                        p_, pk_ = PS[bank], PSk[bank]
                        for kk in range(DC):
                            S.op("pe", lambda e, kk=kk, lf=lf, M=M, p_=p_: e.matmul(p_[0:M, 0:128], lhsT=lf(kk), rhs=wb_[:, kk, 0:128],
                                                                                 start=(kk == 0), stop=(kk == DC - 1)),
                                 reads=[wbk_] + tf(kk), writes=[pk_])
                        t_, tk_ = gettmp()
                        S.op("dve", lambda e, p_=p_, t_=t_, M=M: e.tensor_copy(out=t_[0:M, 0:128], in_=p_[0:M, 0:128]), reads=[pk_], writes=[tk_])
                        if b == 5:
                            S.dma("sp", kp_o[g, :, nb * 128:(nb + 1) * 128], t_[:, 0:128], reads=[tk_], store=True)
                        else:
                            S.dma("sp", ks_new[g, :, nb * 128:(nb + 1) * 128], t_[0:GS, 0:128], reads=[tk_], store=True)
                wstep(wview(w_in, 0, DC, cfg.oK + nb * 128, 128), DC, 128, fKt)


            def conv3(src, w, ci, dst, st_, srck, dstk, wk, stk, stidx=0):
                n = NM - GS - 2 - V0
                d0 = V0 + 2
                S.op("dve", lambda e: e.tensor_scalar(out=dst[:, d0:d0 + n], in0=src[:, V0:V0 + n], scalar1=w[:, ci, 0:1], scalar2=None, op0=ALU.mult),
                     reads=[srck, wk], writes=[dstk])
                for j in (1, 2):
                    S.op("dve", lambda e, j=j: e.scalar_tensor_tensor(out=dst[:, d0:d0 + n], in0=src[:, V0 + j:V0 + j + n], scalar=w[:, ci, j:j + 1],
                                                                      in1=dst[:, d0:d0 + n], op0=ALU.mult, op1=ALU.add),
                         reads=[srck, wk], writes=[dstk])
                s0, s1 = NM - GS, NM
                S.op("dve", lambda e: e.tensor_scalar(out=dst[:, s0:s1], in0=src[:, s0:s1], scalar1=w[:, ci, 2:3], scalar2=None, op0=ALU.mult),
                     reads=[srck, wk], writes=[dstk])
                for j in (0, 1):
                    S.op("dve", lambda e, j=j: e.scalar_tensor_tensor(out=dst[:, s0:s1], in0=(st_[:, ci, j, :] if stidx is not None else st_[:, j, :]), scalar=w[:, ci, j:j + 1],
                                                                      in1=dst[:, s0:s1], op0=ALU.mult, op1=ALU.add),
                         reads=[stk, wk], writes=[dstk])

            for ci in range(CC):
                def fcc(wb_, wbk_, ci=ci):
                    def evac(p_, ti, c0, c1, pk_):
                        S.op("act", lambda e: e.activation(out=ccb[:, c0:c1], in_=p_, func=AF.Copy), reads=[pk_], writes=[ccbk])
                    mm_main(wb_, wbk_, DC, lambda kk, c0, c1: hmain[:, kk, c0:c1], lambda kk: [hmk[kk]], evac)
                wstep(wview(w_in, 0, DC, cfg.oCC + ci * 128, 128), DC, 128, fcc)

                def fch(wb_, wbk_, ci=ci, g=g):
                    def evac(p_, ti, c0, c1, pk_):
                        S.op("dve", lambda e: e.tensor_tensor(out=ub[:, c0:c1], in0=p_, in1=ccb[:, c0:c1], op=ALU.mult), reads=[pk_, ccbk], writes=[ubk])
                    mm_main(wb_, wbk_, DC, lambda kk, c0, c1: hmain[:, kk, c0:c1], lambda kk: [hmk[kk]], evac)
                    S.op("act", lambda e: e.activation(out=ucs[:, ci, :], in_=ub[:, NM - GS - 2:NM], func=AF.Copy), reads=[ubk], writes=[ucsk])
                    conv3(ub, cw, ci, zb, scs, ubk, zbk, cwk, scsk)
                    if ci == CC - 1:
                        S.dma("sp", ucT_o[g], ucs[:], reads=[ucsk], store=True)
                wstep(wview(w_in, 0, DC, cfg.oCH + ci * 128, 128), DC, 128, fch)

                def fcb(wb_, wbk_, ci=ci):
                    def evac(p_, ti, c0, c1, pk_):
                        S.op("dve", lambda e: e.tensor_tensor(out=aT[:, ci, c0:c1], in0=p_, in1=zb[:, c0:c1], op=ALU.mult), reads=[pk_, zbk], writes=[aTk[ci]] + hhk)
                    mm_main(wb_, wbk_, DC, lambda kk, c0, c1: hmain[:, kk, c0:c1], lambda kk: [hmk[kk]], evac)
                wstep(wview(w_in, 0, DC, cfg.oCB + ci * 128, 128), DC, 128, fcb)

            for c in range(AC):
                def fq(wb_, wbk_, c=c):
                    def evac(p_, ti, c0, c1, pk_):
                        S.op("act", lambda e: e.activation(out=QT[:, c, c0:c1], in_=p_, func=AF.Copy), reads=[pk_], writes=[QTk[c]])
                    mm_main(wb_, wbk_, DC, lambda kk, c0, c1: hmain[:, kk, c0:c1], lambda kk: [hmk[kk]], evac)
                wstep(wview(w_in, 0, DC, cfg.oQ + c * 128, 128), DC, 128, fq)

            def attn(g=g):
                qblocks = [(V0 + 2, 2, 126, ("h", 0), ("h", 128), 0, 1, False)]
                for qb in range(4):
                    prev = ("h", 128) if qb == 0 else ("m", NHALO + (qb - 1) * 128)
                    qblocks.append((NHALO + qb * 128, 128, 0, prev, ("m", NHALO + qb * 128), 1 + qb, 2 + qb, qb == 0 and g == 0))

                def kap(k, oh, spec):
                    kind, c0 = spec
                    base = c0 if kind == "h" else NH + c0
                    return Kd[oh:oh + 64, k, base:base + 128]

                def stageA(idx, qc0, nq, i0, kprev, kown, vprev, vown, first, k):
                    W4 = G * nq
                    pT, pTk = pTs[idx % 2], pTks[idx % 2]
                    for ti, kspec in enumerate((kprev, kown)):
                        for par in range(2):
                            sp_, spk_ = PS[2 * ti + par], PSk[2 * ti + par]
                            oh = par * 64
                            for j in range(G // 2):
                                h = G * k + 2 * j + par
                                S.op("pe", lambda e, sp_=sp_, j=j, oh=oh, h=h, kspec=kspec, k=k: e.matmul(
                                    sp_[:, j * nq:(j + 1) * nq], lhsT=kap(k, oh, kspec), rhs=QT[oh:oh + 64, h // 2, qc0:qc0 + nq],
                                    start=True, stop=True), reads=[Kdk[k], QTk[h // 2]], writes=[spk_])
                            bview = bT[:, G * k + par:G * (k + 1):2, ti, i0:i0 + nq]
                            S.op("dve", lambda e, sp_=sp_, ti=ti, par=par, bview=bview: e.scalar_tensor_tensor(
                                out=sT[ti][:, 0:W4].rearrange("p (g i) -> p g i", g=G)[:, par:G:2, :],
                                in0=sp_[:, 0:W4 // 2].rearrange("p (j i) -> p j i", j=G // 2),
                                scalar=scale, in1=bview, op0=ALU.mult, op1=ALU.add), reads=[spk_, bTk], writes=[sTk[ti]])
                        if first and ti == 0:
                            S.op("act", lambda e, ti=ti: e.activation(out=pT[ti][:, 0:W4], in_=sT[ti][:, 0:W4], func=AF.Exp, bias=fm[:, 0:1]),
                                 reads=[sTk[ti], fmk], writes=[pTk[ti]])
                        else:
                            S.op("act", lambda e, ti=ti: e.activation(out=pT[ti][:, 0:W4], in_=sT[ti][:, 0:W4], func=AF.Exp),
                                 reads=[sTk[ti]], writes=[pTk[ti]])

                PVb, PVbk = [PS[4], PS[6]], [PSk[4], PSk[6]]
                dns, dnks = [dn, xs[2][:, 0:512]], [dnk, xsk[2]]

                def stageB1(idx, qc0, nq, i0, kprev, kown, vprev, vown, first, k):
                    W4 = G * nq
                    pT, pTk = pTs[idx % 2], pTks[idx % 2]
                    pv, pvk = PVb[idx % 2], PVbk[idx % 2]
                    dn_, dnk_ = dns[idx % 2], dnks[idx % 2]
                    for ti, vb in enumerate((vprev, vown)):
                        S.op("pe", lambda e, ti=ti, vb=vb, k=k: e.matmul(pv[:, 0:W4], lhsT=Vd[:, vb, k, :], rhs=pT[ti][:, 0:W4],
                                                                         start=(ti == 0), stop=(ti == 1)),
                             reads=[Vdk[vb], pTk[ti]], writes=[pvk])
                    for ti in range(2):
                        S.op("pe", lambda e, ti=ti: e.matmul(PS[5][:, 0:W4], lhsT=ones_bf[:, :], rhs=pT[ti][:, 0:W4],
                                                             start=(ti == 0), stop=(ti == 1)),
                             reads=[onesk, pTk[ti]], writes=[PSk[5]])
                    S.op("dve", lambda e, k=k: e.tensor_tensor(out=dn_[:, 0:W4].rearrange("p (g i) -> p g i", g=G),
                                                               in0=PS[5][:, 0:W4].rearrange("p (g i) -> p g i", g=G),
                                                               in1=bc(esk[:, G * k:G * (k + 1)].rearrange("p (g o) -> p g o", o=1), [128, G, nq]), op=ALU.add),
                         reads=[PSk[5], eskk], writes=[dnk_])
                    S.op("act", lambda e: e.activation(out=dn_[:, 0:W4], in_=dn_[:, 0:W4], func=AF.Ln), reads=[dnk_], writes=[dnk_])
                    S.op("act", lambda e: e.activation(out=dn_[:, 0:W4], in_=dn_[:, 0:W4], func=AF.Exp, scale=-1.0), reads=[dnk_], writes=[dnk_])

                def stageB2(idx, qc0, nq, i0, kprev, kown, vprev, vown, first, k):
                    W4 = G * nq
                    pv, pvk = PVb[idx % 2], PVbk[idx % 2]
                    dn_, dnk_ = dns[idx % 2], dnks[idx % 2]
                    for half in range(2):
                        c_lo = (G * k) // 2
                        pr = slice(half * 64, half * 64 + 64)
                        S.op("dve", lambda e, half=half, pr=pr, c_lo=c_lo: e.tensor_tensor(
                            out=QT[pr, c_lo:c_lo + G // 2, qc0:qc0 + nq],
                            in0=pv[:, 0:W4].rearrange("p (g i) -> p g i", g=G)[pr, half:G:2, :],
                            in1=dn_[:, 0:W4].rearrange("p (g i) -> p g i", g=G)[pr, half:G:2, :], op=ALU.mult),
                            reads=[pvk, dnk_], writes=[QTk[c_lo + j] for j in range(G // 2)])

                items = [qbk + (k,) for qbk in qblocks for k in range(KV)]
                stageA(0, *items[0])
                for it in range(len(items)):
                    if it + 1 < len(items):
                        stageA(it + 1, *items[it + 1])
                    stageB1(it, *items[it])
                    if it >= 1:
                        stageB2(it - 1, *items[it - 1])
                stageB2(len(items) - 1, *items[-1])

                def sample_part():
                    sc0 = NM - GS
                    NB = GS * H
                    for b in range(GS):
                        i = b % 2
                        S.dma("pool", kst16[i][:], kstT[g, b], writes=[kst16k[i]])
                        S.dma("pool", vst16[i][:], vstd[g, b], writes=[vst16k[i]])
                        for par in range(2):
                            bank = 6 if par == 0 else 3
                            oh = par * 64
                            for c in range(H // 2):
                                h = 2 * c + par
                                k = h // G
                                S.op("pe", lambda e, i=i, b=b, c=c, k=k, oh=oh, bank=bank: e.matmul(
                                    PS[bank][:, b * (H // 2) + c:b * (H // 2) + c + 1], lhsT=kst16[i][oh:oh + 64, k, :],
                                    rhs=QT[oh:oh + 64, c, sc0 + b:sc0 + b + 1], start=True, stop=True),
                                    reads=[kst16k[i], QTk[c]], writes=[PSk[bank]])
                            S.op("dve", lambda e, b=b, par=par, bank=bank: e.scalar_tensor_tensor(
                                out=sS[:, b * H:(b + 1) * H].rearrange("p (c t) -> p c t", t=2)[:, :, par],
                                in0=PS[bank][:, b * (H // 2):(b + 1) * (H // 2)],
                                scalar=scale, in1=bS[:, :].rearrange("p (c t) -> p c t", t=2)[:, :, par], op0=ALU.mult, op1=ALU.add),
                                reads=[PSk[bank], bSk], writes=[sSk])
                        S.op("act", lambda e, b=b: e.activation(out=pS[:, b * H:(b + 1) * H], in_=sS[:, b * H:(b + 1) * H], func=AF.Exp), reads=[sSk], writes=[pSk])
                        for k in range(KV):
                            S.op("pe", lambda e, b=b, k=k, i=i: e.matmul(PS[4][:, b * H + G * k:b * H + G * (k + 1)], lhsT=vst16[i][:, k, :],
                                                                         rhs=pS[:, b * H + G * k:b * H + G * (k + 1)], start=True, stop=True),
                                 reads=[vst16k[i], pSk], writes=[PSk[4]])
                    for c in range(AC):
                        kvh = (2 * c) // G
                        S.op("dve", lambda e, c=c, kvh=kvh: e.tensor_tensor(out=prd[:, c, :], in0=QT[:, c, sc0:NM], in1=Kd[:, kvh, NH + sc0:NH + NM], op=ALU.mult),
                             reads=[QTk[c], Kdk[kvh]], writes=[prdk])
                    for t in range(2):
                        S.op("pe", lambda e, t=t: e.matmul(PS[7][:, t * AC * GS:(t + 1) * AC * GS], lhsT=selh[:, t, :], rhs=prd[:, :, :].rearrange("p c b -> p (c b)"),
                                                           start=True, stop=True), reads=[prdk, selk], writes=[PSk[7]])
                    for t in range(2):
                        S.op("dve", lambda e, t=t: e.scalar_tensor_tensor(
                            out=pnb[:, 0:NB].rearrange("p (b c t) -> p b c t", b=GS, t=2)[:, :, :, t],
                            in0=PS[7][:, t * AC * GS:(t + 1) * AC * GS].rearrange("p (c b) -> p b c", b=GS),
                            scalar=scale, in1=bc(bSn[:, :].rearrange("p (o c t) -> p o c t", o=1, t=2)[:, :, :, t], [128, GS, AC]),
                            op0=ALU.mult, op1=ALU.add), reads=[PSk[7], bSnk], writes=[pnbk])
                    S.op("act", lambda e: e.activation(out=pnb[:, 0:NB], in_=pnb[:, 0:NB], func=AF.Exp), reads=[pnbk], writes=[pnbk])
                    S.op("pe", lambda e: e.matmul(PS[5][:, 0:NB], lhsT=ones_bf[:, :], rhs=pS[:, 0:NB], start=True, stop=True), reads=[onesk, pSk], writes=[PSk[5]])
                    v4 = lambda ap: ap.rearrange("p (b k g) -> p b k g", b=GS, k=KV)
                    S.op("dve", lambda e: e.tensor_tensor(out=v4(sS[:, 0:NB]), in0=v4(pnb[:, 0:NB]),
                                                          in1=bc(vnT[:, :, :].rearrange("p k (b o) -> p b k o", o=1), [128, GS, KV, G]), op=ALU.mult),
                         reads=[pnbk, vnTk], writes=[sSk])
                    S.op("dve", lambda e: e.tensor_tensor(out=sS[:, 0:NB], in0=sS[:, 0:NB], in1=PS[4][:, 0:NB], op=ALU.add), reads=[PSk[4], sSk], writes=[sSk])
                    S.op("dve", lambda e: e.tensor_tensor(out=osm[:, 0:NB].rearrange("p (b h) -> p b h", b=GS), in0=pnb[:, 0:NB].rearrange("p (b h) -> p b h", b=GS),
                                                          in1=bc(esk[:, :].rearrange("p (o h) -> p o h", o=1), [128, GS, H]), op=ALU.add),
                         reads=[pnbk, eskk], writes=[osmk])
                    S.op("dve", lambda e: e.tensor_tensor(out=osm[:, 0:NB], in0=osm[:, 0:NB], in1=PS[5][:, 0:NB], op=ALU.add), reads=[PSk[5], osmk], writes=[osmk])
                    S.op("dve", lambda e: e.reciprocal(out=osm[:, 0:NB], in_=osm[:, 0:NB]), reads=[osmk], writes=[osmk])
                    S.op("dve", lambda e: e.tensor_tensor(out=sS[:, 0:NB], in0=sS[:, 0:NB], in1=osm[:, 0:NB], op=ALU.mult), reads=[sSk, osmk], writes=[sSk])
                    for half in range(2):
                        pr = slice(half * 64, half * 64 + 64)
                        S.op("dve", lambda e, half=half, pr=pr: e.tensor_copy(
                            out=QT[pr, :, sc0:NM], in_=sS[:, 0:NB].rearrange("p (b c t) -> p c b t", b=GS, t=2)[pr, :, :, half]),
                            reads=[sSk], writes=QTk)
                sample_part()
                if dbg:
                    dump("oT", QT, QTk, AC, g)
                barrier()
            cstep(attn)

            def dump(name, tile_, toks_, nch, g):
                for c in range(nch):
                    t_, tk_ = gettmp()
                    S.op("dve", lambda e, c=c, t_=t_: e.tensor_copy(out=t_[:, 0:NM], in_=tile_[:, c, :]), reads=[toks_[c]], writes=[tk_])
                    S.dma("sp", dbg_o[name][g, :, c, :], t_[:, 0:NM], reads=[tk_], store=True)

            if dbg:
                cstep(lambda g=g: (dump("hT", hmain, hmk, DC, g), dump("aT", aT, aTk, CC, g)))

            for n in range(DC):
                def fga(wb_, wbk_, n=n):
                    def evac(p_, ti, c0, c1, pk_):
                        S.op("act", lambda e: e.activation(out=ccb[:, c0:c1], in_=p_, func=AF.Sigmoid), reads=[pk_], writes=[ccbk])
                    mm_main(wb_, wbk_, DC, lambda kk, c0, c1: hmain[:, kk, c0:c1], lambda kk: [hmk[kk]], evac)
                wstep(wview(w_in, 0, DC, cfg.oGA + n * 128, 128), DC, 128, fga)

                def fwa(wb_, wbk_, n=n):
                    def evac(p_, ti, c0, c1, pk_):
                        S.op("dve", lambda e: e.tensor_tensor(out=ub[:, c0:c1], in0=p_, in1=ccb[:, c0:c1], op=ALU.mult), reads=[pk_, ccbk], writes=[ubk])
                    mm_main(wb_, wbk_, CC, lambda kk, c0, c1: aT[:, kk, c0:c1], lambda kk: [aTk[kk]], evac)
                wstep(wview(w_a, 0, CC, n * 128, 128), CC, 128, fwa)

                def fgb(wb_, wbk_, n=n):
                    def evac(p_, ti, c0, c1, pk_):
                        S.op("act", lambda e: e.activation(out=ccb[:, c0:c1], in_=p_, func=AF.Sigmoid), reads=[pk_], writes=[ccbk])
                    mm_main(wb_, wbk_, DC, lambda kk, c0, c1: hmain[:, kk, c0:c1], lambda kk: [hmk[kk]], evac)
                wstep(wview(w_in, 0, DC, cfg.oGB + n * 128, 128), DC, 128, fgb)

                def fwb(wb_, wbk_, n=n):
                    def evac(p_, ti, c0, c1, pk_):
                        S.op("dve", lambda e: e.tensor_tensor(out=zb[:, c0:c1], in0=p_, in1=ccb[:, c0:c1], op=ALU.mult), reads=[pk_, ccbk], writes=[zbk])
                        S.op("dve", lambda e: e.tensor_tensor(out=mT[:, n, c0:c1], in0=zb[:, c0:c1], in1=ub[:, c0:c1], op=ALU.add), reads=[zbk, ubk], writes=[mTk[n]])
                    mm_main(wb_, wbk_, AC, lambda kk, c0, c1: QT[:, kk, c0:c1], lambda kk: [QTk[kk]], evac)
                wstep(wview(w_b, 0, AC, n * 128, 128), AC, 128, fwb)

            def after_merge(g=g):
                if dbg:
                    dump("mT", mT, mTk, DC, g)
                barrier()
                for c in range(NXS):
                    S.dma("sp", xs[c][:, 0:NM], xT[g, c * 128:(c + 1) * 128, NH:NX], writes=[xsk[c]])
            cstep(after_merge)

            for n in range(DC):
                def fo(wb_, wbk_, n=n, g=g):
                    i = n % NXS

                    def evac(p_, ti, c0, c1, pk_):
                        S.op("dve", lambda e: e.tensor_tensor(out=xmid[:, n, c0:c1], in0=p_, in1=xs[i][:, c0:c1], op=ALU.add), reads=[pk_, xsk[i]], writes=[xmk[n]])
                    mm_main(wb_, wbk_, DC, lambda kk, c0, c1: mT[:, kk, c0:c1], lambda kk: [mTk[kk]], evac)
                    if n + NXS < DC:
                        S.dma("sp", xs[i][:, 0:NM], xT[g, (n + NXS) * 128:(n + NXS + 1) * 128, NH:NX], writes=[xsk[i]])
                    ssq_acc(xmid[:, n, V0:NM], xmk[n], n, DC, TT, V0, [6, 7])
                wstep(wview(w_o, 0, DC, n * 128, 128), DC, 128, fo)

            def norm2(g=g):
                rstd_fin(TT, [6, 7])
                if dbg:
                    dump("xm", xmid, xmk, DC, g)
                barrier()
                for c in range(DC):
                    S.op("dve", lambda e, c=c: e.scalar_tensor_tensor(out=h2T[:, c, V0:NM], in0=xmid[:, c, V0:NM], scalar=g2[:, c:c + 1], in1=rstd[:, V0:NM],
                                                                      op0=ALU.mult, op1=ALU.mult), reads=[xmk[c], g2k, rstdk], writes=[h2k[c]])
                S.op("pool", lambda e: e.memset(zb[:], 0.0), writes=[zbk])
            cstep(norm2)

            for (p0_, pn_) in cfg.parts:
                for jl in range(pn_):
                    j = p0_ + jl

                    def fg(wb_, wbk_, j=j, g=g):
                        r = j % 2
                        S.dma("sp", sfr[r][:], sfT[g, :, j], writes=[sfrk[r]])

                        def evac(p_, ti, c0, c1, pk_):
                            S.op("act", lambda e: e.activation(out=ub[:, c0:c1], in_=p_, func=AF.Copy), reads=[pk_], writes=[ubk])
                        mm_main(wb_, wbk_, DC, lambda kk, c0, c1: h2T[:, kk, c0:c1], lambda kk: [h2k[kk]], evac)
                        S.op("act", lambda e: e.activation(out=gcr[r][:, :], in_=ub[:, NM - GS - 2:NM], func=AF.Copy), reads=[ubk], writes=[gcrk[r]])
                        S.dma("sp", gcT_o[g, :, j, :], gcr[r][:, :], reads=[gcrk[r]], store=True)
                        conv3(ub, fw, j, zb, sfr[r], ubk, zbk, fwk, sfrk[r], stidx=None)
                        S.op("act", lambda e: e.activation(out=ccb[:, :], in_=zb[:, :], func=AF.Silu, bias=fb[:, j:j + 1]), reads=[zbk, fbk], writes=[ccbk])
                    wstep(wview(w_g, 0, DC, j * 128, 128), DC, 128, fg)

                    def fu(wb_, wbk_, jl=jl):
                        def evac(p_, ti, c0, c1, pk_):
                            S.op("dve", lambda e: e.tensor_tensor(out=fT[:, jl, c0:c1], in0=p_, in1=ccb[:, c0:c1], op=ALU.mult), reads=[pk_, ccbk], writes=[fTk[jl]])
                        mm_main(wb_, wbk_, DC, lambda kk, c0, c1: h2T[:, kk, c0:c1], lambda kk: [h2k[kk]], evac)
                    wstep(wview(w_u, 0, DC, j * 128, 128), DC, 128, fu)
                for n in range(DC):
                    def fd(wb_, wbk_, n=n, pn_=pn_, last=(p0_ == cfg.parts[-1][0])):
                        def evac(p_, ti, c0, c1, pk_):
                            S.op("dve", lambda e: e.tensor_tensor(out=xmid[:, n, c0:c1], in0=p_, in1=xmid[:, n, c0:c1], op=ALU.add), reads=[pk_], writes=[xmk[n]])
                        mm_main(wb_, wbk_, pn_, lambda kk, c0, c1: fT[:, kk, c0:c1], lambda kk: [fTk[kk]], evac)
                        if last:
                            ssq_acc(xmid[:, n, V0:NM], xmk[n], n, DC, TT, V0, [6, 7])
                    wstep(wview(w_d, p0_ * 128, pn_, n * 128, 128), pn_, 128, fd)

            def fin(g=g):
                rstd_fin(TT, [6, 7])
                for c in range(DC):
                    i = c % NTMP
                    S.op("dve", lambda e, c=c, i=i: e.scalar_tensor_tensor(out=ysl[i][:, V0:NM], in0=xmid[:, c, V0:NM], scalar=gf[:, c:c + 1], in1=rstd[:, V0:NM],
                                                                           op0=ALU.mult, op1=ALU.mult), reads=[xmk[c], gfk, rstdk], writes=[yslk[i]])
                    S.dma("sp", yT[g, c * 128:(c + 1) * 128, :], ysl[i][:, NHALO:NM], reads=[yslk[i]], store=True)
                barrier(drain=False)
            cstep(fin)

        if maxsteps is not None:
            del steps[maxsteps:]
        run_steps()
        S.emit()
        nops = {e: len(S.ops[e]) for e in S.ENG}
    return nc, nops


def _t5_bucket(dist):
    max_exact = N_BUCKETS // 2
    d = np.maximum(dist, 0)
    df = np.maximum(d, 1).astype(np.float32)
    large = max_exact + (np.log(df / np.float32(max_exact)) / np.float32(math.log(MAX_DISTANCE / max_exact))
                         * np.float32(N_BUCKETS - max_exact)).astype(np.int32)
    large = np.minimum(large, N_BUCKETS - 1)
    return np.where(d < max_exact, d, large)


def fm_vec(v):
    return np.ascontiguousarray(v.reshape(-1, 128).T)


def make_inputs(cfg, inp):
    f32 = np.float32
    D, H, KV, FF = cfg.D, cfg.H, cfg.KV, cfg.FF
    x_prompt = np.asarray(inp["x_prompt"], f32)
    x_sample = np.asarray(inp["x_sample"], f32)[:, 0, :]
    sk = np.asarray(inp["state_k_window"], f32)[0].reshape(DEC_BATCH, 128, cfg.KW)
    sv = np.asarray(inp["state_v_window"], f32)[0].reshape(DEC_BATCH, 128, cfg.KW)
    sc = np.asarray(inp["state_conv"], f32)[0]
    sf = np.asarray(inp["state_ffn_conv"], f32)[0]
    rel_bias = np.asarray(inp["rel_bias"], f32)
    sinks = np.asarray(inp["sinks"], f32)[0]

    shared = {
        "w_in": np.asarray(inp["w_in"], f32)[0], "w_a": np.asarray(inp["w_branch_a"], f32)[0],
        "w_b": np.asarray(inp["w_branch_b"], f32)[0], "w_o": np.asarray(inp["w_out"], f32)[0],
        "w_g": np.asarray(inp["w_ffn_gate"], f32)[0], "w_u": np.asarray(inp["w_ffn_up"], f32)[0],
        "w_d": np.asarray(inp["w_ffn_down"], f32)[0],
        "g1T": fm_vec(np.asarray(inp["attn_norm_g"], f32)[0]), "g2T": fm_vec(np.asarray(inp["ffn_norm_g"], f32)[0]),
        "gfT": fm_vec(np.asarray(inp["final_norm_g"], f32)),
        "cwT": np.ascontiguousarray(np.asarray(inp["conv_w"], f32)[0].reshape(3, -1, 128).transpose(2, 1, 0)),
        "fwT": np.ascontiguousarray(np.asarray(inp["ffn_conv_w"], f32)[0].reshape(3, -1, 128).transpose(2, 1, 0)),
        "fbT": fm_vec(np.asarray(inp["ffn_conv_b"], f32)[0]),
        "sinkb": np.ascontiguousarray(np.broadcast_to(sinks[None, :], (128, H))),
    }
    jj = np.arange(128)[:, None]
    ii = np.arange(128)[None, :]
    bt = np.empty((128, H, 2, 128), f32)
    dprev = ii + 128 - jj
    down = ii - jj
    for v, dist in enumerate((dprev, down)):
        valid = (dist >= 0) & (dist <= WINDOW)
        vals = rel_bias[_t5_bucket(dist)]
        vals = np.where(valid[:, :, None], vals, f32(NEG))
        bt[:, :, v, :] = vals.transpose(0, 2, 1)
    shared["biasT"] = bt
    ds = 128 - np.arange(128)
    shared["biasS"] = np.ascontiguousarray(rel_bias[_t5_bucket(ds)])
    shared["biasSn"] = np.ascontiguousarray(np.broadcast_to(rel_bias[_t5_bucket(np.zeros(1, np.int64))], (128, H)))

    in_maps = []
    for c in range(NCORES):
        b, half = c // 2, c % 2
        m = dict(shared)
        xg = np.zeros((NGRP, NH + NM, D), f32)
        for g in range(NGRP):
            s = half * 1024 + g * GP
            if s >= NH:
                xg[g, 0:NH] = x_prompt[b, s - NH:s]
                xg[g, NH + V0:NH + NHALO] = x_prompt[b, s - (NHALO - V0):s]
            xg[g, NH + NHALO:NH + NHALO + GP] = x_prompt[b, s:s + GP]
            xg[g, NH + NHALO + GP:] = x_sample[c * 16 + g * GS:c * 16 + (g + 1) * GS]
        m["xT"] = np.ascontiguousarray(xg.transpose(0, 2, 1))
        m["fmask"] = np.full((128, 1), NEG if half == 0 else 0.0, f32)
        sl = slice(c * 16, c * 16 + 16)
        m["st_k"], m["st_v"] = sk[sl], sv[sl]
        m["st_c"], m["st_f"] = sc[sl], sf[sl]
        m["scT"] = np.ascontiguousarray(sc[sl].reshape(NGRP, GS, 2, -1, 128).transpose(0, 4, 3, 2, 1))
        m["sfT"] = np.ascontiguousarray(sf[sl].reshape(NGRP, GS, 2, -1, 128).transpose(0, 4, 3, 2, 1))
        k4 = sk[sl].reshape(NGRP, GS, 128, KV, HD)
        kt = k4.transpose(0, 1, 4, 3, 2)
        m["kstT"] = np.ascontiguousarray(np.concatenate([kt, kt], axis=2))
        v4 = sv[sl].reshape(NGRP, GS, 128, KV, HD)
        m["vstd"] = np.ascontiguousarray(np.concatenate([v4, v4], axis=4))
        in_maps.append(m)
    return in_maps


def assemble(cfg, res, inp):
    f32 = np.float32
    D, KV, FF = cfg.D, cfg.KV, cfg.FF
    y_prompt = np.empty((BATCH, SEQ, D), f32)
    y_sample = np.empty((DEC_BATCH, 1, D), f32)
    kp = np.empty((1, BATCH, 128, KV, HD), f32)
    vp = np.empty((1, BATCH, 128, KV, HD), f32)
    cp = np.empty((1, BATCH, 2, cfg.CW), f32)
    fp = np.empty((1, BATCH, 2, FF), f32)
    ksn = np.empty((1, DEC_BATCH, 128, KV, HD), f32)
    vsn = np.empty((1, DEC_BATCH, 128, KV, HD), f32)
    cs = np.empty((1, DEC_BATCH, 2, cfg.CW), f32)
    fs = np.empty((1, DEC_BATCH, 2, FF), f32)
    for c in range(NCORES):
        r = res[c]
        b, half = c // 2, c % 2
        for g in range(NGRP):
            s = half * 1024 + g * GP
            yt = r["yT"][g]
            y_prompt[b, s:s + GP] = yt[:, :GP].T
            y_sample[c * 16 + g * GS:c * 16 + (g + 1) * GS, 0] = yt[:, GP:].T
            sl = slice(c * 16 + g * GS, c * 16 + (g + 1) * GS)
            uc = r["ucT_o"][g].transpose(2, 1, 0).reshape(2 + GS, -1)
            gc = r["gcT_o"][g].transpose(2, 1, 0).reshape(2 + GS, -1)
            cs[0, sl, 1] = uc[2:]
            fs[0, sl, 1] = gc[2:]
            ksn[0, sl, 127] = r["ks_new"][g].reshape(GS, KV, HD)
            vsn[0, sl, 127] = r["vs_new"][g].reshape(GS, KV, HD)
            if half == 1 and g == NGRP - 1:
                kp[0, b] = r["kp_o"][g].reshape(128, KV, HD)
                vp[0, b] = r["vp_o"][g].reshape(128, KV, HD)
                cp[0, b] = uc[:2]
                fp[0, b] = gc[:2]
        sl = slice(c * 16, c * 16 + 16)
        ksn[0, sl, :127] = r["ks_o"].reshape(16, 127, KV, HD)
        vsn[0, sl, :127] = r["vs_o"].reshape(16, 127, KV, HD)
        cs[0, sl, 0] = r["c0_o"]
        fs[0, sl, 0] = r["f0_o"]
    return (y_prompt, y_sample, kp, vp, cp, fp, ksn, vsn, cs, fs)


def kernel(**inputs):
    cfg = REAL
    nc, _ = build(cfg)
    in_maps = make_inputs(cfg, inputs)
    res = run_bass_kernel_spmd(nc, in_maps, core_ids=list(range(NCORES)))
    return assemble(cfg, res.results, inputs)
```

```python
import bisect
import math
from contextlib import ExitStack

import numpy as np
import concourse.bass as bass
import concourse.mybir as mybir
from concourse.bass_utils import run_bass_kernel_spmd

F32 = mybir.dt.float32
BF16 = mybir.dt.bfloat16
AF = mybir.ActivationFunctionType
ALU = mybir.AluOpType
AX = mybir.AxisListType

NCORES = 8
BATCH = 4
SEQ = 2048
DEC_BATCH = 128
WINDOW = 128
HD = 64
N_BUCKETS = 32
MAX_DISTANCE = 128
EPS = 1e-5
NEG = -1e30
NGRP = 2
GP = 512
GS = 8
NHALO = 32
V0 = 28
NM = NHALO + GP + GS
NH = 256
TT = [(V0, V0 + (NM - V0) // 2), (V0 + (NM - V0) // 2, NM)]
HT = [(0, NH)]


class Cfg:
    def __init__(self, D=4096, H=32, KV=8, FF=11008, partmax=22):
        self.D, self.H, self.KV, self.FF = D, H, KV, FF
        self.G = H // KV
        self.AW = H * HD
        self.KW = KV * HD
        self.CW = D // 2
        self.DC, self.AC, self.CC, self.FC = D // 128, self.AW // 128, self.CW // 128, FF // 128
        self.INW = 3 * self.CW + self.AW + 2 * self.KW + 2 * D
        self.oCB, self.oCC, self.oCH = 0, self.CW, 2 * self.CW
        self.oQ = 3 * self.CW
        self.oK = self.oQ + self.AW
        self.oV = self.oK + self.KW
        self.oGA = self.oV + self.KW
        self.oGB = self.oGA + D
        self.KMAX = max(self.DC, self.AC, self.CC)
        np_ = -(-self.FC // partmax)
        base, rem = divmod(self.FC, np_)
        self.parts = []
        s = 0
        for i in range(np_):
            n = base + (1 if i < rem else 0)
            self.parts.append((s, n))
            s += n
        self.PMAX = max(n for _, n in self.parts)
        assert self.PMAX <= self.KMAX
        assert self.AC == self.CC == self.DC // 2


REAL = Cfg()


class Tok:
    __slots__ = ("name", "w", "r", "dsem", "dcnt")

    def __init__(self, name):
        self.name = name
        self.w = None
        self.r = []
        self.dsem = None
        self.dcnt = 0


class Sched:
    ENG = ["pe", "act", "dve", "pool", "sp"]

    def __init__(self, nc, stack):
        self.nc = nc
        self.stack = stack
        self.ops = {e: [] for e in self.ENG}
        self.sigseq = {e: [] for e in self.ENG}
        self.sem = {e: stack.enter_context(nc.semaphore("s_" + e)) for e in self.ENG}
        self.waited = {}
        self.store_evs = []
        self.dma_toks = []

    def tok(self, name):
        return Tok(name)

    def toks(self, name, n):
        return [Tok(f"{name}{i}") for i in range(n)]

    def _resolve(self, ev):
        if ev[0] == "d":
            return ev[1], ev[2]
        _, eng, seq = ev
        ss = self.sigseq[eng]
        i = bisect.bisect_left(ss, seq)
        if i == len(ss):
            last = len(self.ops[eng]) - 1
            rec = self.ops[eng][last]
            assert not rec[2] and not rec[3], (eng, seq, last)
            rec[2] = True
            ss.append(last)
            i = len(ss) - 1
        return self.sem[eng], i + 1

    def _waits(self, eng, reads, writes):
        deps = []
        for t in reads:
            if t.w is not None:
                deps.append(t.w)
        for t in writes:
            if t.w is not None:
                deps.append(t.w)
            deps.extend(t.r)
        need = {}
        for ev in deps:
            if ev[0] == "c" and ev[1] == eng and eng == "pe":
                continue
            sem, val = self._resolve(ev)
            k = id(sem)
            if k not in need or need[k][1] < val:
                need[k] = (sem, val)
        out = []
        for k, (sem, val) in need.items():
            if self.waited.get((eng, k), 0) >= val:
                continue
            self.waited[(eng, k)] = val
            out.append((sem, val))
        return out

    def op(self, eng, fn, reads=(), writes=(), signal=None):
        if signal is None:
            signal = eng != "pe"
        waits = self._waits(eng, reads, writes)
        seq = len(self.ops[eng])
        self.ops[eng].append([waits, fn, signal, False, None])
        if signal:
            self.sigseq[eng].append(seq)
        ev = ("c", eng, seq)
        for t in reads:
            t.r.append(ev)
        for t in writes:
            t.w = ev
            t.r = []
        return ev

    def dma(self, q, out, in_, reads=(), writes=(), semtok=None, store=False, **kw):
        if semtok is None:
            semtok = writes[0] if writes else reads[0]
        if semtok.dsem is None:
            semtok.dsem = self.stack.enter_context(self.nc.semaphore("d_" + semtok.name))
            self.dma_toks.append(semtok)
        waits = self._waits(q, reads, writes)
        semtok.dcnt += 16
        ev = ("d", semtok.dsem, semtok.dcnt)

        def fn(e, out=out, in_=in_, kw=kw):
            return e.dma_start(out=out, in_=in_, **kw)

        self.ops[q].append([waits, fn, False, True, semtok.dsem])
        for t in reads:
            t.r.append(ev)
        for t in writes:
            t.w = ev
            t.r = []
        if store:
            self.store_evs.append(ev)
        return ev

    def barrier(self, carriers):
        arr = {}
        for eng in self.ENG:
            t = Tok("bar_" + eng)
            self.op(eng, carriers[eng], writes=[t], signal=True)
            arr[eng] = t
        alltoks = list(arr.values())
        for eng in self.ENG:
            waits = self._waits(eng, alltoks, [])
            for dt_ in self.dma_toks:
                k = id(dt_.dsem)
                if self.waited.get((eng, k), 0) < dt_.dcnt:
                    self.waited[(eng, k)] = dt_.dcnt
                    waits.append((dt_.dsem, dt_.dcnt))
            t = Tok("barw_" + eng)
            self.op(eng, carriers[eng], writes=[t], signal=True)
            self.ops[eng][-1][0] = waits + self.ops[eng][-1][0]

    def run_engine(self, eng, e):
        sem = self.sem[eng]
        for waits, fn, signal, is_dma, dsem in self.ops[eng]:
            for s, v in waits:
                e.wait_ge(s, v)
            ins = fn(e)
            if is_dma:
                ins.then_inc(dsem, 16)
            elif signal:
                ins.then_inc(sem, 1)
        if eng == "sp":
            need = {}
            for ev in self.store_evs:
                k = id(ev[1])
                if k not in need or need[k][1] < ev[2]:
                    need[k] = (ev[1], ev[2])
            for s, v in need.values():
                e.wait_ge(s, v)

    def emit(self):
        nc = self.nc
        with nc.Block() as block:
            if self.ops["pe"]:
                block.tensor(lambda e: self.run_engine("pe", e))
            if self.ops["act"]:
                block.scalar(lambda e: self.run_engine("act", e))
            if self.ops["dve"]:
                block.vector(lambda e: self.run_engine("dve", e))
            if self.ops["pool"]:
                block.gpsimd(lambda e: self.run_engine("pool", e))
            block.sync(lambda e: self.run_engine("sp", e))


def bc(ap, shape):
    return ap.broadcast_to(list(shape))


def build(cfg, dbg=False, maxsteps=None):
    D, H, KV, FF, G = cfg.D, cfg.H, cfg.KV, cfg.FF, cfg.G
    DC, AC, CC, FC = cfg.DC, cfg.AC, cfg.CC, cfg.FC
    KMAX = cfg.KMAX
    NX = NH + NM
    scale = HD ** -0.5

    nc = bass.Bass("TRN2", target_bir_lowering=False)

    def din(name, shape):
        return nc.dram_tensor(name, list(shape), F32, kind="ExternalInput").ap()

    def dout(name, shape):
        return nc.dram_tensor(name, list(shape), F32, kind="ExternalOutput").ap()

    xT = din("xT", [NGRP, D, NX])
    w_in = din("w_in", [D, cfg.INW])
    w_a = din("w_a", [cfg.CW, D])
    w_b = din("w_b", [cfg.AW, D])
    w_o = din("w_o", [D, D])
    w_g = din("w_g", [D, FF])
    w_u = din("w_u", [D, FF])
    w_d = din("w_d", [FF, D])
    g1T = din("g1T", [128, DC])
    g2T = din("g2T", [128, DC])
    gfT = din("gfT", [128, DC])
    cwT = din("cwT", [128, CC, 3])
    fwT = din("fwT", [128, FC, 3])
    fbT = din("fbT", [128, FC])
    biasT = din("biasT", [128, H, 2, 128])
    fmask = din("fmask", [128, 1])
    biasS = din("biasS", [128, H])
    biasSn = din("biasSn", [128, H])
    sinkb = din("sinkb", [128, H])
    scT = din("scT", [NGRP, 128, CC, 2, GS])
    sfT = din("sfT", [NGRP, 128, FC, 2, GS])
    kstT = din("kstT", [NGRP, GS, 128, KV, 128])
    vstd = din("vstd", [NGRP, GS, 128, KV, 128])
    st_k = din("st_k", [NGRP * GS, 128, cfg.KW])
    st_v = din("st_v", [NGRP * GS, 128, cfg.KW])
    st_c = din("st_c", [NGRP * GS, 2, cfg.CW])
    st_f = din("st_f", [NGRP * GS, 2, FF])

    yT = dout("yT", [NGRP, D, GP + GS])
    kp_o = dout("kp_o", [NGRP, 128, cfg.KW])
    vp_o = dout("vp_o", [NGRP, 128, cfg.KW])
    ks_new = dout("ks_new", [NGRP, GS, cfg.KW])
    vs_new = dout("vs_new", [NGRP, GS, cfg.KW])
    ucT_o = dout("ucT_o", [NGRP, 128, CC, 2 + GS])
    gcT_o = dout("gcT_o", [NGRP, 128, FC, 2 + GS])
    ks_o = dout("ks_o", [NGRP * GS, 127, cfg.KW])
    vs_o = dout("vs_o", [NGRP * GS, 127, cfg.KW])
    c0_o = dout("c0_o", [NGRP * GS, cfg.CW])
    f0_o = dout("f0_o", [NGRP * GS, FF])
    dbg_o = {}
    if dbg:
        dbg_o["hT"] = dout("dbg_hT", [NGRP, 128, DC, NM])
        dbg_o["aT"] = dout("dbg_aT", [NGRP, 128, CC, NM])
        dbg_o["oT"] = dout("dbg_oT", [NGRP, 128, AC, NM])
        dbg_o["mT"] = dout("dbg_mT", [NGRP, 128, DC, NM])
        dbg_o["xm"] = dout("dbg_xm", [NGRP, 128, DC, NM])

    with ExitStack() as st:
        S = Sched(nc, st)

        def sb(name, shape, dt=F32):
            return st.enter_context(nc.sbuf_tensor(name, list(shape), dt))

        def ps(name, shape, dt=F32):
            return st.enter_context(nc.psum_tensor(name, list(shape), dt))

        WA = DC * NM
        arenaA = sb("arenaA", [128, WA])
        xmid = arenaA[:, :].rearrange("p (c n) -> p c n", c=DC)
        o1 = DC * NM // 2
        o2 = o1 + DC * (NM // 2) // 2
        hmain = arenaA[:, 0:o1].bitcast(BF16).rearrange("p (c n) -> p c n", c=DC)
        hhalo = arenaA[:, o1:o1 + DC * NH // 2].bitcast(BF16).rearrange("p (c n) -> p c n", c=DC)
        aT = arenaA[:, o1:o1 + CC * NM // 2].bitcast(BF16).rearrange("p (c n) -> p c n", c=CC)
        QT = arenaA[:, o2:o2 + AC * NM // 2].bitcast(BF16).rearrange("p (c n) -> p c n", c=AC)
        assert o2 + AC * NM // 2 <= WA and o1 + CC * NM // 2 <= o2 and o1 + DC * NH // 2 <= o2

        wK = KV * NX // 2
        wV = 6 * KV * 128 // 2
        wX = max(H * 256, DC * NM // 2)
        wB = max(wK + wV + wX, DC * NM // 2 + cfg.PMAX * NM // 2)
        arenaB = sb("arenaB", [128, wB])
        Kd = arenaB[:, 0:wK].bitcast(BF16).rearrange("p (k n) -> p k n", k=KV)
        Vd = arenaB[:, wK:wK + wV].bitcast(BF16).rearrange("p (b k d) -> p b k d", b=6, k=KV)
        bT = arenaB[:, wK + wV:wK + wV + H * 256].rearrange("p (h v i) -> p h v i", h=H, v=2)
        mT = arenaB[:, wK + wV:wK + wV + DC * NM // 2].bitcast(BF16).rearrange("p (c n) -> p c n", c=DC)
        h2T = arenaB[:, 0:DC * NM // 2].bitcast(BF16).rearrange("p (c n) -> p c n", c=DC)
        fT = arenaB[:, DC * NM // 2:DC * NM // 2 + cfg.PMAX * NM // 2].bitcast(BF16).rearrange("p (c n) -> p c n", c=cfg.PMAX)

        hmk = S.toks("hm", DC)
        hhk = S.toks("hh", DC)
        aTk = S.toks("aT", CC)
        QTk = S.toks("QT", AC)
        Kdk = S.toks("Kd", KV)
        Vdk = S.toks("Vd", 6)
        bTk = S.tok("bT")
        mTk = S.toks("mT", DC)
        xmk = S.toks("xm", DC)
        h2k = S.toks("h2", DC)
        fTk = S.toks("fT", cfg.PMAX)

        NWB = 4
        wbf = [sb(f"wbf{i}", [128, KMAX, 128], BF16) for i in range(NWB)]
        wbk = S.toks("wbf", NWB)

        def const_load(name, src, shape, dt=F32):
            t = sb(name, shape, dt)
            k = S.tok(name)
            S.dma("sp", t[:], src, writes=[k])
            return t, k

        g1, g1k = const_load("g1", g1T, [128, DC])
        g2, g2k = const_load("g2", g2T, [128, DC])
        gf, gfk = const_load("gf", gfT, [128, DC])
        cw, cwk = const_load("cw", cwT, [128, CC, 3])
        fw, fwk = const_load("fw", fwT, [128, FC, 3])
        fb, fbk = const_load("fb", fbT, [128, FC])
        fm, fmk = const_load("fm", fmask, [128, 1])
        bS, bSk = const_load("bS", biasS, [128, H])
        bSn, bSnk = const_load("bSn", biasSn, [128, H])
        esk, eskk = const_load("esk", sinkb, [128, H])
        S.op("act", lambda e: e.activation(out=esk[:], in_=esk[:], func=AF.Exp), reads=[eskk], writes=[eskk])

        ones_bf = sb("ones_bf", [128, 128], BF16)
        onesk = S.tok("ones")
        S.op("dve", lambda e: e.memset(ones_bf[:], 1.0), writes=[onesk])
        ones_f = sb("ones_f", [1, 128])
        S.op("dve", lambda e: e.memset(ones_f[:], 1.0), writes=[onesk])
        scr = {e_: sb("scr_" + e_, [128, 8]) for e_ in ["act", "dve", "pool"]}
        for e_ in ["act", "dve", "pool"]:
            S.op("dve", lambda e, e_=e_: e.memset(scr[e_][:], 0.0), writes=[S.tok("scr" + e_)])
        scr_sp_src = sb("scr_sp_src", [1, 16])
        scr_sp_dst = sb("scr_sp_dst", [1, 16])
        S.op("pool", lambda e: e.memset(scr_sp_src[:], 0.0), writes=[S.tok("x")])

        PS = [ps(f"ps{i}", [128, 512]) for i in range(8)]
        PSk = S.toks("ps", 8)

        spk = S.tok("spbar")
        carriers = {
            "pe": lambda e: e.matmul(PS[7][0:1, 0:1], lhsT=ones_f[0:1, 0:1], rhs=ones_f[0:1, 0:1], start=True, stop=True),
            "act": lambda e: e.activation(out=scr["act"][:, 0:1], in_=scr["act"][:, 1:2], func=AF.Copy),
            "dve": lambda e: e.memset(scr["dve"][:, 0:1], 0.0),
            "pool": lambda e: e.memset(scr["pool"][:, 0:1], 0.0),
            "sp": None,
        }

        def barrier(drain=True):
            arr = {}
            for eng in ["pe", "act", "dve", "pool"]:
                t = Tok("bar_" + eng)
                rw = [PSk[7]] if eng == "pe" else []
                S.op(eng, carriers[eng], writes=[t] + rw, signal=True)
                arr[eng] = t
            alltoks = list(arr.values())
            for eng in ["pe", "act", "dve", "pool"]:
                rw = [PSk[7]] if eng == "pe" else []
                S.op(eng, carriers[eng], reads=alltoks, writes=rw, signal=True)
                extra = []
                for dt_ in (S.dma_toks if drain else []):
                    k = id(dt_.dsem)
                    if S.waited.get((eng, k), 0) < dt_.dcnt:
                        S.waited[(eng, k)] = dt_.dcnt
                        extra.append((dt_.dsem, dt_.dcnt))
                S.ops[eng][-1][0] = S.ops[eng][-1][0] + extra
            S.dma("sp", scr_sp_dst[:], scr_sp_src[:], reads=alltoks, writes=[spk])
            extra = []
            for dt_ in (S.dma_toks if drain else []):
                k = id(dt_.dsem)
                if S.waited.get(("sp", k), 0) < dt_.dcnt and dt_ is not spk:
                    S.waited[("sp", k)] = dt_.dcnt
                    extra.append((dt_.dsem, dt_.dcnt))
            S.ops["sp"][-1][0] = S.ops["sp"][-1][0] + extra

        steps = []

        def wstep(wap, kc, ncol, fn, dup=False):
            steps.append(dict(w=wap, kc=kc, ncol=ncol, dup=dup, fn=fn))

        def cstep(fn):
            steps.append(dict(w=None, fn=fn))

        def wview(W, r0, kc, c0, ncol):
            return W[r0:r0 + kc * 128, c0:c0 + ncol].rearrange("(c p) n -> p c n", p=128)

        cast_rr = [0]

        def run_steps():
            wsteps = [s_ for s_ in steps if s_["w"] is not None]
            for i, s_ in enumerate(wsteps):
                s_["wi"] = i
            issued_dma = [0]

            def do_dma(upto):
                while issued_dma[0] < min(upto, len(wsteps)):
                    j = issued_dma[0]
                    s_ = wsteps[j]
                    wl = j % NWB
                    kc, ncol = s_["kc"], s_["ncol"]
                    for r in range(2 if s_["dup"] else 1):
                        S.dma("pool", wbf[wl][:, 0:kc, r * ncol:(r + 1) * ncol], s_["w"], writes=[wbk[wl]])
                    issued_dma[0] += 1

            for s_ in steps:
                if s_["w"] is None:
                    s_["fn"]()
                    continue
                i = s_["wi"]
                do_dma(i + NWB)
                s_["fn"](wbf[i % NWB], wbk[i % NWB])
            steps.clear()

        mm_rr = [0]

        def mm_main(wb_, wbk_, kc, rhs_fn, rhs_toks, evac, tiles=TT, M=128):
            base = (mm_rr[0] % 2) * 3
            mm_rr[0] += 1
            for ti, (c0, c1) in enumerate(tiles):
                p_, pk_ = PS[base + ti], PSk[base + ti]
                for k in range(kc):
                    S.op("pe", lambda e, p_=p_, k=k, c0=c0, c1=c1: e.matmul(
                        p_[0:M, 0:c1 - c0], lhsT=wb_[:, k, 0:M], rhs=rhs_fn(k, c0, c1), start=(k == 0), stop=(k == kc - 1)),
                        reads=[wbk_] + rhs_toks(k), writes=[pk_])
                evac(p_[0:M, 0:c1 - c0], ti, c0, c1, pk_)

        NTMP = 3
        tmpf = [sb(f"tmpf{i}", [128, NM + 4]) for i in range(NTMP)]
        tmpk = S.toks("tmpf", NTMP)
        tmp_rr = [0]

        def gettmp():
            i = tmp_rr[0] % NTMP
            tmp_rr[0] += 1
            return tmpf[i], tmpk[i]

        NXS = 4
        xs = [sb(f"xs{i}", [128, NX]) for i in range(NXS)]
        xsk = S.toks("xs", NXS)
        sqb = [sb(f"sq{i}", [128, NX], BF16) for i in range(2)]
        sqk = S.toks("sq", 2)
        rstd = sb("rstd", [128, NX])
        rstdk = S.tok("rstd")
        S.op("pool", lambda e: e.memset(rstd[:], 1.0), writes=[rstdk])
        ucs = sb("ucs", [128, CC, 2 + GS])
        ucsk = S.tok("ucs")
        gcr = [sb(f"gcr{i}", [128, 2 + GS]) for i in range(2)]
        gcrk = S.toks("gcr", 2)
        scs = sb("scs", [128, CC, 2, GS])
        scsk = S.tok("scs")
        sfr = [sb(f"sfr{i}", [128, 2, GS]) for i in range(2)]
        sfrk = S.toks("sfr", 2)
        ysl = [t_[:, 0:NM] for t_ in tmpf]
        yslk = tmpk
        sT = [x_[:, 0:512] for x_ in xs[0:2]]
        sTk = xsk[0:2]
        pTb = [None, None]
        pTs = [[q_[:, 0:512] for q_ in sqb], pTb]
        pTks = [sqk, [None, None]]
        dn = rstd[:, 0:512]
        dnk = rstdk
        kst16 = [sb(f"kst16_{i}", [128, KV, 128], BF16) for i in range(2)]
        kst16k = S.toks("kst16", 2)
        vst16 = [sb(f"vst16_{i}", [128, KV, 128], BF16) for i in range(2)]
        vst16k = S.toks("vst16", 2)
        vnT = sb("vnT", [128, KV, GS])
        vnTk = S.tok("vnT")
        pS = sb("pS", [128, GS * H], BF16)
        pSk = S.tok("pS")
        pnb, pnbk = tmpf[2][:, 0:GS * H], tmpk[2]
        prd = sb("prd", [128, AC, GS])
        prdk = S.tok("prd")
        assert cfg.KW // 128 <= AC
        vst, vstk = prd[:, 0:cfg.KW // 128, :], prdk
        selh = sb("selh", [128, 2, 128])
        selk = S.tok("selh")
        S.op("pool", lambda e: e.memset(selh[:], 0.0), writes=[selk])
        S.op("pool", lambda e: e.memset(selh[0:64, 0, :], 1.0), writes=[selk])
        S.op("pool", lambda e: e.memset(selh[64:128, 1, :], 1.0), writes=[selk])
        assert GS * H <= NM and NTMP >= 3
        sS, sSk = tmpf[0][:, 0:GS * H], tmpk[0]
        osm, osmk = tmpf[1][:, 0:GS * H], tmpk[1]
        ccb = sb("ccb", [128, NM])
        ccbk = S.tok("ccb")
        ub = sb("ub", [128, NM])
        ubk = S.tok("ub")
        zb = sb("zb", [128, NM])
        zbk = S.tok("zb")
        pTb[0], pTb[1] = ccb[:, 0:256].bitcast(BF16), ub[:, 0:256].bitcast(BF16)
        kvtm, kvtmk = zb[:, 0:512], zbk
        pTks[1][0], pTks[1][1] = ccbk, ubk
        S.op("pool", lambda e: e.memset(zb[:], 0.0), writes=[zbk])

        d2dk = S.tok("d2d")
        S.dma("sp", ks_o[:, :, :], st_k[:, 1:128, :], writes=[d2dk], store=True)
        S.dma("sp", vs_o[:, :, :], st_v[:, 1:128, :], writes=[d2dk], store=True)
        S.dma("sp", c0_o[:, :], st_c[:, 1, :], writes=[d2dk], store=True)
        S.dma("sp", f0_o[:, :], st_f[:, 1, :], writes=[d2dk], store=True)

        def ssq_acc(src_ap, src_tok, c, nchunks, tiles, col0, banks):
            i = c % 2
            ncols = src_ap.shape[-1]
            S.op("act", lambda e: e.activation(out=sqb[i][:, col0:col0 + ncols], in_=src_ap, func=AF.Square),
                 reads=[src_tok], writes=[sqk[i]])
            for bi, (c0, c1) in enumerate(tiles):
                S.op("pe", lambda e, bi=bi, c0=c0, c1=c1: e.matmul(PS[banks[bi]][:, 0:c1 - c0], lhsT=ones_bf[:, :], rhs=sqb[i][:, c0:c1],
                                                                   start=(c == 0), stop=(c == nchunks - 1)),
                     reads=[sqk[i], onesk], writes=[PSk[banks[bi]]])

        def rstd_fin(tiles, banks):
            for bi, (c0, c1) in enumerate(tiles):
                S.op("act", lambda e, bi=bi, c0=c0, c1=c1: e.activation(out=rstd[:, c0:c1], in_=PS[banks[bi]][:, 0:c1 - c0], func=AF.Sqrt,
                                                                       scale=1.0 / D, bias=EPS),
                     reads=[PSk[banks[bi]]], writes=[rstdk])
            n = tiles[-1][1]
            S.op("dve", lambda e: e.reciprocal(out=rstd[:, 0:n], in_=rstd[:, 0:n]), reads=[rstdk], writes=[rstdk])

        XT3 = [(0, 260), (260, 520), (520, NX)]

        for g in range(NGRP):
            def p0(g=g):
                for c in range(DC):
                    i = c % NXS
                    S.dma("sp", xs[i][:, :], xT[g, c * 128:(c + 1) * 128, :], writes=[xsk[i]])
                    ssq_acc(xs[i][:, :], xsk[i], c, DC, XT3, 0, [0, 1, 2])
                rstd_fin(XT3, [0, 1, 2])
                for c in range(DC):
                    i = c % NXS
                    S.dma("sp", xs[i][:, :], xT[g, c * 128:(c + 1) * 128, :], writes=[xsk[i]])
                    S.op("dve", lambda e, c=c, i=i: e.scalar_tensor_tensor(out=hhalo[:, c, :], in0=xs[i][:, 0:NH], scalar=g1[:, c:c + 1],
                                                                           in1=rstd[:, 0:NH], op0=ALU.mult, op1=ALU.mult),
                         reads=[xsk[i], g1k, rstdk], writes=[hhk[c]])
                    S.op("dve", lambda e, c=c, i=i: e.scalar_tensor_tensor(out=hmain[:, c, :], in0=xs[i][:, NH:NX], scalar=g1[:, c:c + 1],
                                                                           in1=rstd[:, NH:NX], op0=ALU.mult, op1=ALU.mult),
                         reads=[xsk[i], g1k, rstdk], writes=[hmk[c]])
                S.dma("sp", scs[:], scT[g], writes=[scsk])
                S.dma("sp", bT[:], biasT, writes=[bTk])
            cstep(p0)

            for k in range(KV):
                def fK(wb_, wbk_, k=k):
                    def evac(p_, ti, c0, c1, pk_, k=k):
                        S.op("act", lambda e: e.activation(out=Kd[:, k, NH + c0:NH + c1], in_=p_, func=AF.Copy), reads=[pk_], writes=[Kdk[k]])
                    mm_main(wb_, wbk_, DC, lambda kk, c0, c1: hmain[:, kk, c0:c1], lambda kk: [hmk[kk]], evac)

                    def evach(p_, ti, c0, c1, pk_, k=k):
                        S.op("act", lambda e: e.activation(out=Kd[:, k, c0:c1], in_=p_, func=AF.Copy), reads=[pk_], writes=[Kdk[k]])
                    mm_main(wb_, wbk_, DC, lambda kk, c0, c1: hhalo[:, kk, c0:c1], lambda kk: [hhk[kk]], evach, tiles=HT)
                wstep(wview(w_in, 0, DC, cfg.oK + k * HD, HD), DC, HD, fK, dup=True)
            def blk_rhs(b):
                if b < 2:
                    return (lambda kk, b=b: hhalo[:, kk, b * 128:(b + 1) * 128]), (lambda kk: [hhk[kk]]), 128
                if b < 6:
                    return (lambda kk, b=b: hmain[:, kk, NHALO + (b - 2) * 128:NHALO + (b - 1) * 128]), (lambda kk: [hmk[kk]]), 128
                return (lambda kk: hmain[:, kk, NM - GS:NM]), (lambda kk: [hmk[kk]]), GS

            for nb in range(cfg.KW // 128):
                def fV(wb_, wbk_, nb=nb, g=g):
                    bank = (mm_rr[0] % 2) * 3
                    mm_rr[0] += 1
                    p_, pk_ = PS[bank], PSk[bank]
                    for kk in range(DC):
                        S.op("pe", lambda e, kk=kk, p_=p_: e.matmul(p_[:, 0:GS], lhsT=wb_[:, kk, 0:128], rhs=hmain[:, kk, NM - GS:NM],
                                                                     start=(kk == 0), stop=(kk == DC - 1)),
                             reads=[wbk_, hmk[kk]], writes=[pk_])
                    S.op("act", lambda e, p_=p_: e.activation(out=vst[:, nb, :], in_=p_[:, 0:GS], func=AF.Copy), reads=[pk_], writes=[vstk])
                    if nb == cfg.KW // 128 - 1:
                        for r in range(2):
                            for half in range(2):
                                S.dma("sp", vnT[half * 64:(half + 1) * 64, r:KV:2, :], vst[r * 64:(r + 1) * 64, :, :], reads=[vstk], writes=[vnTk])
                    for b in range(7):
                        lf, tf, M = blk_rhs(b)
                        bank = (mm_rr[0] % 2) * 3
                        mm_rr[0] += 1
                        p_, pk_ = PS[bank], PSk[bank]
                        for kk in range(DC):
                            S.op("pe", lambda e, kk=kk, lf=lf, M=M, p_=p_: e.matmul(p_[0:M, 0:128], lhsT=lf(kk), rhs=wb_[:, kk, 0:128],
                                                                                 start=(kk == 0), stop=(kk == DC - 1)),
                                 reads=[wbk_] + tf(kk), writes=[pk_])
                        if b < 6:
                            src = p_[:, 0:128].rearrange("p (k d) -> p k d", k=2)
                            for r in range(2):
                                S.op("act", lambda e, b=b, r=r, src=src: e.activation(out=Vd[:, b, 2 * nb:2 * nb + 2, r * 64:(r + 1) * 64], in_=src, func=AF.Copy),
                                     reads=[pk_], writes=[Vdk[b]])
                        if b == 5:
                            S.op("act", lambda e, p_=p_: e.activation(out=kvtm[:, nb * 128:(nb + 1) * 128], in_=p_[:, 0:128], func=AF.Copy), reads=[pk_], writes=[kvtmk])
                            if nb == cfg.KW // 128 - 1:
                                S.dma("sp", vp_o[g, :, :], kvtm[:, 0:cfg.KW], reads=[kvtmk], store=True)
                        if b == 6:
                            t_, tk_ = gettmp()
                            S.op("dve", lambda e, p_=p_, t_=t_: e.tensor_copy(out=t_[0:GS, 0:128], in_=p_[0:GS, 0:128]), reads=[pk_], writes=[tk_])
                            S.dma("sp", vs_new[g, :, nb * 128:(nb + 1) * 128], t_[0:GS, 0:128], reads=[tk_], store=True)
                wstep(wview(w_in, 0, DC, cfg.oV + nb * 128, 128), DC, 128, fV)
            for nb in range(cfg.KW // 128):
                def fKt(wb_, wbk_, nb=nb, g=g):
                    for b in (5, 6):
                        lf, tf, M = blk_rhs(b)
                        bank = (mm_rr[0] % 2) * 3
                        mm_rr[0] += 1
                        p_, pk_ = PS[bank], PSk[bank]
                        for kk in range(DC):
                            S.op("pe", lambda e, kk=kk, lf=lf, M=M, p_=p_: e.matmul(p_[0:M, 0:128], lhsT=lf(kk), rhs=wb_[:, kk, 0:128],
                                                                                 start=(kk == 0), stop=(kk == DC - 1)),
                                 reads=[wbk_] + tf(kk), writes=[pk_])
                        t_, tk_ = gettmp()
                        S.op("dve", lambda e, p_=p_, t_=t_, M=M: e.tensor_copy(out=t_[0:M, 0:128], in_=p_[0:M, 0:128]), reads=[pk_], writes=[tk_])
                        if b == 5:
                            S.dma("sp", kp_o[g, :, nb * 128:(nb + 1) * 128], t_[:, 0:128], reads=[tk_], store=True)
                        else:
                            S.dma("sp", ks_new[g, :, nb * 128:(nb + 1) * 128], t_[0:GS, 0:128], reads=[tk_], store=True)
                wstep(wview(w_in, 0, DC, cfg.oK + nb * 128, 128), DC, 128, fKt)


            def conv3(src, w, ci, dst, st_, srck, dstk, wk, stk, stidx=0):
                n = NM - GS - 2 - V0
                d0 = V0 + 2
                S.op("dve", lambda e: e.tensor_scalar(out=dst[:, d0:d0 + n], in0=src[:, V0:V0 + n], scalar1=w[:, ci, 0:1], scalar2=None, op0=ALU.mult),
                     reads=[srck, wk], writes=[dstk])
                for j in (1, 2):
                    S.op("dve", lambda e, j=j: e.scalar_tensor_tensor(out=dst[:, d0:d0 + n], in0=src[:, V0 + j:V0 + j + n], scalar=w[:, ci, j:j + 1],
                                                                      in1=dst[:, d0:d0 + n], op0=ALU.mult, op1=ALU.add),
                         reads=[srck, wk], writes=[dstk])
                s0, s1 = NM - GS, NM
                S.op("dve", lambda e: e.tensor_scalar(out=dst[:, s0:s1], in0=src[:, s0:s1], scalar1=w[:, ci, 2:3], scalar2=None, op0=ALU.mult),
                     reads=[srck, wk], writes=[dstk])
                for j in (0, 1):
                    S.op("dve", lambda e, j=j: e.scalar_tensor_tensor(out=dst[:, s0:s1], in0=(st_[:, ci, j, :] if stidx is not None else st_[:, j, :]), scalar=w[:, ci, j:j + 1],
                                                                      in1=dst[:, s0:s1], op0=ALU.mult, op1=ALU.add),
                         reads=[stk, wk], writes=[dstk])

            for ci in range(CC):
                def fcc(wb_, wbk_, ci=ci):
                    def evac(p_, ti, c0, c1, pk_):
                        S.op("act", lambda e: e.activation(out=ccb[:, c0:c1], in_=p_, func=AF.Copy), reads=[pk_], writes=[ccbk])
                    mm_main(wb_, wbk_, DC, lambda kk, c0, c1: hmain[:, kk, c0:c1], lambda kk: [hmk[kk]], evac)
                wstep(wview(w_in, 0, DC, cfg.oCC + ci * 128, 128), DC, 128, fcc)

                def fch(wb_, wbk_, ci=ci, g=g):
                    def evac(p_, ti, c0, c1, pk_):
                        S.op("dve", lambda e: e.tensor_tensor(out=ub[:, c0:c1], in0=p_, in1=ccb[:, c0:c1], op=ALU.mult), reads=[pk_, ccbk], writes=[ubk])
                    mm_main(wb_, wbk_, DC, lambda kk, c0, c1: hmain[:, kk, c0:c1], lambda kk: [hmk[kk]], evac)
                    S.op("act", lambda e: e.activation(out=ucs[:, ci, :], in_=ub[:, NM - GS - 2:NM], func=AF.Copy), reads=[ubk], writes=[ucsk])
                    conv3(ub, cw, ci, zb, scs, ubk, zbk, cwk, scsk)
                    if ci == CC - 1:
                        S.dma("sp", ucT_o[g], ucs[:], reads=[ucsk], store=True)
                wstep(wview(w_in, 0, DC, cfg.oCH + ci * 128, 128), DC, 128, fch)

                def fcb(wb_, wbk_, ci=ci):
                    def evac(p_, ti, c0, c1, pk_):
                        S.op("dve", lambda e: e.tensor_tensor(out=aT[:, ci, c0:c1], in0=p_, in1=zb[:, c0:c1], op=ALU.mult), reads=[pk_, zbk], writes=[aTk[ci]] + hhk)
                    mm_main(wb_, wbk_, DC, lambda kk, c0, c1: hmain[:, kk, c0:c1], lambda kk: [hmk[kk]], evac)
                wstep(wview(w_in, 0, DC, cfg.oCB + ci * 128, 128), DC, 128, fcb)

            for c in range(AC):
                def fq(wb_, wbk_, c=c):
                    def evac(p_, ti, c0, c1, pk_):
                        S.op("act", lambda e: e.activation(out=QT[:, c, c0:c1], in_=p_, func=AF.Copy), reads=[pk_], writes=[QTk[c]])
                    mm_main(wb_, wbk_, DC, lambda kk, c0, c1: hmain[:, kk, c0:c1], lambda kk: [hmk[kk]], evac)
                wstep(wview(w_in, 0, DC, cfg.oQ + c * 128, 128), DC, 128, fq)

            def attn(g=g):
                qblocks = [(V0 + 2, 2, 126, ("h", 0), ("h", 128), 0, 1, False)]
                for qb in range(4):
                    prev = ("h", 128) if qb == 0 else ("m", NHALO + (qb - 1) * 128)
                    qblocks.append((NHALO + qb * 128, 128, 0, prev, ("m", NHALO + qb * 128), 1 + qb, 2 + qb, qb == 0 and g == 0))

                def kap(k, oh, spec):
                    kind, c0 = spec
                    base = c0 if kind == "h" else NH + c0
                    return Kd[oh:oh + 64, k, base:base + 128]

                def stageA(idx, qc0, nq, i0, kprev, kown, vprev, vown, first, k):
                    W4 = G * nq
                    pT, pTk = pTs[idx % 2], pTks[idx % 2]
                    for ti, kspec in enumerate((kprev, kown)):
                        for par in range(2):
                            sp_, spk_ = PS[2 * ti + par], PSk[2 * ti + par]
                            oh = par * 64
                            for j in range(G // 2):
                                h = G * k + 2 * j + par
                                S.op("pe", lambda e, sp_=sp_, j=j, oh=oh, h=h, kspec=kspec, k=k: e.matmul(
                                    sp_[:, j * nq:(j + 1) * nq], lhsT=kap(k, oh, kspec), rhs=QT[oh:oh + 64, h // 2, qc0:qc0 + nq],
                                    start=True, stop=True), reads=[Kdk[k], QTk[h // 2]], writes=[spk_])
                            bview = bT[:, G * k + par:G * (k + 1):2, ti, i0:i0 + nq]
                            S.op("dve", lambda e, sp_=sp_, ti=ti, par=par, bview=bview: e.scalar_tensor_tensor(
                                out=sT[ti][:, 0:W4].rearrange("p (g i) -> p g i", g=G)[:, par:G:2, :],
                                in0=sp_[:, 0:W4 // 2].rearrange("p (j i) -> p j i", j=G // 2),
                                scalar=scale, in1=bview, op0=ALU.mult, op1=ALU.add), reads=[spk_, bTk], writes=[sTk[ti]])
                        if first and ti == 0:
                            S.op("act", lambda e, ti=ti: e.activation(out=pT[ti][:, 0:W4], in_=sT[ti][:, 0:W4], func=AF.Exp, bias=fm[:, 0:1]),
                                 reads=[sTk[ti], fmk], writes=[pTk[ti]])
                        else:
                            S.op("act", lambda e, ti=ti: e.activation(out=pT[ti][:, 0:W4], in_=sT[ti][:, 0:W4], func=AF.Exp),
                                 reads=[sTk[ti]], writes=[pTk[ti]])

                PVb, PVbk = [PS[4], PS[6]], [PSk[4], PSk[6]]
                dns, dnks = [dn, xs[2][:, 0:512]], [dnk, xsk[2]]

                def stageB1(idx, qc0, nq, i0, kprev, kown, vprev, vown, first, k):
                    W4 = G * nq
                    pT, pTk = pTs[idx % 2], pTks[idx % 2]
                    pv, pvk = PVb[idx % 2], PVbk[idx % 2]
                    dn_, dnk_ = dns[idx % 2], dnks[idx % 2]
                    for ti, vb in enumerate((vprev, vown)):
                        S.op("pe", lambda e, ti=ti, vb=vb, k=k: e.matmul(pv[:, 0:W4], lhsT=Vd[:, vb, k, :], rhs=pT[ti][:, 0:W4],
                                                                         start=(ti == 0), stop=(ti == 1)),
                             reads=[Vdk[vb], pTk[ti]], writes=[pvk])
                    for ti in range(2):
                        S.op("pe", lambda e, ti=ti: e.matmul(PS[5][:, 0:W4], lhsT=ones_bf[:, :], rhs=pT[ti][:, 0:W4],
                                                             start=(ti == 0), stop=(ti == 1)),
                             reads=[onesk, pTk[ti]], writes=[PSk[5]])
                    S.op("dve", lambda e, k=k: e.tensor_tensor(out=dn_[:, 0:W4].rearrange("p (g i) -> p g i", g=G),
                                                               in0=PS[5][:, 0:W4].rearrange("p (g i) -> p g i", g=G),
                                                               in1=bc(esk[:, G * k:G * (k + 1)].rearrange("p (g o) -> p g o", o=1), [128, G, nq]), op=ALU.add),
                         reads=[PSk[5], eskk], writes=[dnk_])
                    S.op("act", lambda e: e.activation(out=dn_[:, 0:W4], in_=dn_[:, 0:W4], func=AF.Ln), reads=[dnk_], writes=[dnk_])
                    S.op("act", lambda e: e.activation(out=dn_[:, 0:W4], in_=dn_[:, 0:W4], func=AF.Exp, scale=-1.0), reads=[dnk_], writes=[dnk_])

                def stageB2(idx, qc0, nq, i0, kprev, kown, vprev, vown, first, k):
                    W4 = G * nq
                    pv, pvk = PVb[idx % 2], PVbk[idx % 2]
                    dn_, dnk_ = dns[idx % 2], dnks[idx % 2]
                    for half in range(2):
                        c_lo = (G * k) // 2
                        pr = slice(half * 64, half * 64 + 64)
                        S.op("dve", lambda e, half=half, pr=pr, c_lo=c_lo: e.tensor_tensor(
                            out=QT[pr, c_lo:c_lo + G // 2, qc0:qc0 + nq],
                            in0=pv[:, 0:W4].rearrange("p (g i) -> p g i", g=G)[pr, half:G:2, :],
                            in1=dn_[:, 0:W4].rearrange("p (g i) -> p g i", g=G)[pr, half:G:2, :], op=ALU.mult),
                            reads=[pvk, dnk_], writes=[QTk[c_lo + j] for j in range(G // 2)])

                items = [qbk + (k,) for qbk in qblocks for k in range(KV)]
                stageA(0, *items[0])
                for it in range(len(items)):
                    if it + 1 < len(items):
                        stageA(it + 1, *items[it + 1])
                    stageB1(it, *items[it])
                    if it >= 1:
                        stageB2(it - 1, *items[it - 1])
                stageB2(len(items) - 1, *items[-1])

                def sample_part():
                    sc0 = NM - GS
                    NB = GS * H
                    for b in range(GS):
                        i = b % 2
                        S.dma("pool", kst16[i][:], kstT[g, b], writes=[kst16k[i]])
                        S.dma("pool", vst16[i][:], vstd[g, b], writes=[vst16k[i]])
                        for par in range(2):
                            bank = 6 if par == 0 else 3
                            oh = par * 64
                            for c in range(H // 2):
                                h = 2 * c + par
                                k = h // G
                                S.op("pe", lambda e, i=i, b=b, c=c, k=k, oh=oh, bank=bank: e.matmul(
                                    PS[bank][:, b * (H // 2) + c:b * (H // 2) + c + 1], lhsT=kst16[i][oh:oh + 64, k, :],
                                    rhs=QT[oh:oh + 64, c, sc0 + b:sc0 + b + 1], start=True, stop=True),
                                    reads=[kst16k[i], QTk[c]], writes=[PSk[bank]])
                            S.op("dve", lambda e, b=b, par=par, bank=bank: e.scalar_tensor_tensor(
                                out=sS[:, b * H:(b + 1) * H].rearrange("p (c t) -> p c t", t=2)[:, :, par],
                                in0=PS[bank][:, b * (H // 2):(b + 1) * (H // 2)],
                                scalar=scale, in1=bS[:, :].rearrange("p (c t) -> p c t", t=2)[:, :, par], op0=ALU.mult, op1=ALU.add),
                                reads=[PSk[bank], bSk], writes=[sSk])
                        S.op("act", lambda e, b=b: e.activation(out=pS[:, b * H:(b + 1) * H], in_=sS[:, b * H:(b + 1) * H], func=AF.Exp), reads=[sSk], writes=[pSk])
                        for k in range(KV):
                            S.op("pe", lambda e, b=b, k=k, i=i: e.matmul(PS[4][:, b * H + G * k:b * H + G * (k + 1)], lhsT=vst16[i][:, k, :],
                                                                         rhs=pS[:, b * H + G * k:b * H + G * (k + 1)], start=True, stop=True),
                                 reads=[vst16k[i], pSk], writes=[PSk[4]])
                    for c in range(AC):
                        kvh = (2 * c) // G
                        S.op("dve", lambda e, c=c, kvh=kvh: e.tensor_tensor(out=prd[:, c, :], in0=QT[:, c, sc0:NM], in1=Kd[:, kvh, NH + sc0:NH + NM], op=ALU.mult),
                             reads=[QTk[c], Kdk[kvh]], writes=[prdk])
                    for t in range(2):
                        S.op("pe", lambda e, t=t: e.matmul(PS[7][:, t * AC * GS:(t + 1) * AC * GS], lhsT=selh[:, t, :], rhs=prd[:, :, :].rearrange("p c b -> p (c b)"),
                                                           start=True, stop=True), reads=[prdk, selk], writes=[PSk[7]])
                    for t in range(2):
                        S.op("dve", lambda e, t=t: e.scalar_tensor_tensor(
                            out=pnb[:, 0:NB].rearrange("p (b c t) -> p b c t", b=GS, t=2)[:, :, :, t],
                            in0=PS[7][:, t * AC * GS:(t + 1) * AC * GS].rearrange("p (c b) -> p b c", b=GS),
                            scalar=scale, in1=bc(bSn[:, :].rearrange("p (o c t) -> p o c t", o=1, t=2)[:, :, :, t], [128, GS, AC]),
                            op0=ALU.mult, op1=ALU.add), reads=[PSk[7], bSnk], writes=[pnbk])
                    S.op("act", lambda e: e.activation(out=pnb[:, 0:NB], in_=pnb[:, 0:NB], func=AF.Exp), reads=[pnbk], writes=[pnbk])
                    S.op("pe", lambda e: e.matmul(PS[5][:, 0:NB], lhsT=ones_bf[:, :], rhs=pS[:, 0:NB], start=True, stop=True), reads=[onesk, pSk], writes=[PSk[5]])
                    v4 = lambda ap: ap.rearrange("p (b k g) -> p b k g", b=GS, k=KV)
                    S.op("dve", lambda e: e.tensor_tensor(out=v4(sS[:, 0:NB]), in0=v4(pnb[:, 0:NB]),
                                                          in1=bc(vnT[:, :, :].rearrange("p k (b o) -> p b k o", o=1), [128, GS, KV, G]), op=ALU.mult),
                         reads=[pnbk, vnTk], writes=[sSk])
                    S.op("dve", lambda e: e.tensor_tensor(out=sS[:, 0:NB], in0=sS[:, 0:NB], in1=PS[4][:, 0:NB], op=ALU.add), reads=[PSk[4], sSk], writes=[sSk])
                    S.op("dve", lambda e: e.tensor_tensor(out=osm[:, 0:NB].rearrange("p (b h) -> p b h", b=GS), in0=pnb[:, 0:NB].rearrange("p (b h) -> p b h", b=GS),
                                                          in1=bc(esk[:, :].rearrange("p (o h) -> p o h", o=1), [128, GS, H]), op=ALU.add),
                         reads=[pnbk, eskk], writes=[osmk])
                    S.op("dve", lambda e: e.tensor_tensor(out=osm[:, 0:NB], in0=osm[:, 0:NB], in1=PS[5][:, 0:NB], op=ALU.add), reads=[PSk[5], osmk], writes=[osmk])
                    S.op("dve", lambda e: e.reciprocal(out=osm[:, 0:NB], in_=osm[:, 0:NB]), reads=[osmk], writes=[osmk])
                    S.op("dve", lambda e: e.tensor_tensor(out=sS[:, 0:NB], in0=sS[:, 0:NB], in1=osm[:, 0:NB], op=ALU.mult), reads=[sSk, osmk], writes=[sSk])
                    for half in range(2):
                        pr = slice(half * 64, half * 64 + 64)
                        S.op("dve", lambda e, half=half, pr=pr: e.tensor_copy(
                            out=QT[pr, :, sc0:NM], in_=sS[:, 0:NB].rearrange("p (b c t) -> p c b t", b=GS, t=2)[pr, :, :, half]),
                            reads=[sSk], writes=QTk)
                sample_part()
                if dbg:
                    dump("oT", QT, QTk, AC, g)
                barrier()
            cstep(attn)

            def dump(name, tile_, toks_, nch, g):
                for c in range(nch):
                    t_, tk_ = gettmp()
                    S.op("dve", lambda e, c=c, t_=t_: e.tensor_copy(out=t_[:, 0:NM], in_=tile_[:, c, :]), reads=[toks_[c]], writes=[tk_])
                    S.dma("sp", dbg_o[name][g, :, c, :], t_[:, 0:NM], reads=[tk_], store=True)

            if dbg:
                cstep(lambda g=g: (dump("hT", hmain, hmk, DC, g), dump("aT", aT, aTk, CC, g)))

            for n in range(DC):
                def fga(wb_, wbk_, n=n):
                    def evac(p_, ti, c0, c1, pk_):
                        S.op("act", lambda e: e.activation(out=ccb[:, c0:c1], in_=p_, func=AF.Sigmoid), reads=[pk_], writes=[ccbk])
                    mm_main(wb_, wbk_, DC, lambda kk, c0, c1: hmain[:, kk, c0:c1], lambda kk: [hmk[kk]], evac)
                wstep(wview(w_in, 0, DC, cfg.oGA + n * 128, 128), DC, 128, fga)

                def fwa(wb_, wbk_, n=n):
                    def evac(p_, ti, c0, c1, pk_):
                        S.op("dve", lambda e: e.tensor_tensor(out=ub[:, c0:c1], in0=p_, in1=ccb[:, c0:c1], op=ALU.mult), reads=[pk_, ccbk], writes=[ubk])
                    mm_main(wb_, wbk_, CC, lambda kk, c0, c1: aT[:, kk, c0:c1], lambda kk: [aTk[kk]], evac)
                wstep(wview(w_a, 0, CC, n * 128, 128), CC, 128, fwa)

                def fgb(wb_, wbk_, n=n):
                    def evac(p_, ti, c0, c1, pk_):
                        S.op("act", lambda e: e.activation(out=ccb[:, c0:c1], in_=p_, func=AF.Sigmoid), reads=[pk_], writes=[ccbk])
                    mm_main(wb_, wbk_, DC, lambda kk, c0, c1: hmain[:, kk, c0:c1], lambda kk: [hmk[kk]], evac)
                wstep(wview(w_in, 0, DC, cfg.oGB + n * 128, 128), DC, 128, fgb)

                def fwb(wb_, wbk_, n=n):
                    def evac(p_, ti, c0, c1, pk_):
                        S.op("dve", lambda e: e.tensor_tensor(out=zb[:, c0:c1], in0=p_, in1=ccb[:, c0:c1], op=ALU.mult), reads=[pk_, ccbk], writes=[zbk])
                        S.op("dve", lambda e: e.tensor_tensor(out=mT[:, n, c0:c1], in0=zb[:, c0:c1], in1=ub[:, c0:c1], op=ALU.add), reads=[zbk, ubk], writes=[mTk[n]])
                    mm_main(wb_, wbk_, AC, lambda kk, c0, c1: QT[:, kk, c0:c1], lambda kk: [QTk[kk]], evac)
                wstep(wview(w_b, 0, AC, n * 128, 128), AC, 128, fwb)

            def after_merge(g=g):
                if dbg:
                    dump("mT", mT, mTk, DC, g)
                barrier()
                for c in range(NXS):
                    S.dma("sp", xs[c][:, 0:NM], xT[g, c * 128:(c + 1) * 128, NH:NX], writes=[xsk[c]])
            cstep(after_merge)

            for n in range(DC):
                def fo(wb_, wbk_, n=n, g=g):
                    i = n % NXS

                    def evac(p_, ti, c0, c1, pk_):
                        S.op("dve", lambda e: e.tensor_tensor(out=xmid[:, n, c0:c1], in0=p_, in1=xs[i][:, c0:c1], op=ALU.add), reads=[pk_, xsk[i]], writes=[xmk[n]])
                    mm_main(wb_, wbk_, DC, lambda kk, c0, c1: mT[:, kk, c0:c1], lambda kk: [mTk[kk]], evac)
                    if n + NXS < DC:
                        S.dma("sp", xs[i][:, 0:NM], xT[g, (n + NXS) * 128:(n + NXS + 1) * 128, NH:NX], writes=[xsk[i]])
                    if n >= 1:
                        ssq_acc(xmid[:, n - 1, V0:NM], xmk[n - 1], n - 1, DC, TT, V0, [6, 7])
                wstep(wview(w_o, 0, DC, n * 128, 128), DC, 128, fo)

            def norm2(g=g):
                ssq_acc(xmid[:, DC - 1, V0:NM], xmk[DC - 1], DC - 1, DC, TT, V0, [6, 7])
                rstd_fin(TT, [6, 7])
                if dbg:
                    dump("xm", xmid, xmk, DC, g)
                barrier()
                for c in range(DC):
                    S.op("dve", lambda e, c=c: e.scalar_tensor_tensor(out=h2T[:, c, V0:NM], in0=xmid[:, c, V0:NM], scalar=g2[:, c:c + 1], in1=rstd[:, V0:NM],
                                                                      op0=ALU.mult, op1=ALU.mult), reads=[xmk[c], g2k, rstdk], writes=[h2k[c]])
                S.op("pool", lambda e: e.memset(zb[:], 0.0), writes=[zbk])
            cstep(norm2)

            for (p0_, pn_) in cfg.parts:
                for jl in range(pn_):
                    j = p0_ + jl

                    def fg(wb_, wbk_, j=j, g=g):
                        r = j % 2
                        S.dma("sp", sfr[r][:], sfT[g, :, j], writes=[sfrk[r]])

                        def evac(p_, ti, c0, c1, pk_):
                            S.op("act", lambda e: e.activation(out=ub[:, c0:c1], in_=p_, func=AF.Copy), reads=[pk_], writes=[ubk])
                        mm_main(wb_, wbk_, DC, lambda kk, c0, c1: h2T[:, kk, c0:c1], lambda kk: [h2k[kk]], evac)
                        S.op("act", lambda e: e.activation(out=gcr[r][:, :], in_=ub[:, NM - GS - 2:NM], func=AF.Copy), reads=[ubk], writes=[gcrk[r]])
                        S.dma("sp", gcT_o[g, :, j, :], gcr[r][:, :], reads=[gcrk[r]], store=True)
                        conv3(ub, fw, j, zb, sfr[r], ubk, zbk, fwk, sfrk[r], stidx=None)
                        S.op("act", lambda e: e.activation(out=ccb[:, :], in_=zb[:, :], func=AF.Silu, bias=fb[:, j:j + 1]), reads=[zbk, fbk], writes=[ccbk])
                    wstep(wview(w_g, 0, DC, j * 128, 128), DC, 128, fg)

                    def fu(wb_, wbk_, jl=jl):
                        def evac(p_, ti, c0, c1, pk_):
                            S.op("dve", lambda e: e.tensor_tensor(out=fT[:, jl, c0:c1], in0=p_, in1=ccb[:, c0:c1], op=ALU.mult), reads=[pk_, ccbk], writes=[fTk[jl]])
                        mm_main(wb_, wbk_, DC, lambda kk, c0, c1: h2T[:, kk, c0:c1], lambda kk: [h2k[kk]], evac)
                    wstep(wview(w_u, 0, DC, j * 128, 128), DC, 128, fu)
                for n in range(DC):
                    def fd(wb_, wbk_, n=n, pn_=pn_, last=(p0_ == cfg.parts[-1][0])):
                        def evac(p_, ti, c0, c1, pk_):
                            S.op("dve", lambda e: e.tensor_tensor(out=xmid[:, n, c0:c1], in0=p_, in1=xmid[:, n, c0:c1], op=ALU.add), reads=[pk_], writes=[xmk[n]])
                        mm_main(wb_, wbk_, pn_, lambda kk, c0, c1: fT[:, kk, c0:c1], lambda kk: [fTk[kk]], evac)
                        if last and n >= 1:
                            ssq_acc(xmid[:, n - 1, V0:NM], xmk[n - 1], n - 1, DC, TT, V0, [6, 7])
                    wstep(wview(w_d, p0_ * 128, pn_, n * 128, 128), pn_, 128, fd)

            def fin(g=g):
                ssq_acc(xmid[:, DC - 1, V0:NM], xmk[DC - 1], DC - 1, DC, TT, V0, [6, 7])
                rstd_fin(TT, [6, 7])
                for c in range(DC):
                    i = c % NTMP
                    S.op("dve", lambda e, c=c, i=i: e.scalar_tensor_tensor(out=ysl[i][:, V0:NM], in0=xmid[:, c, V0:NM], scalar=gf[:, c:c + 1], in1=rstd[:, V0:NM],
                                                                           op0=ALU.mult, op1=ALU.mult), reads=[xmk[c], gfk, rstdk], writes=[yslk[i]])
                    S.dma("sp", yT[g, c * 128:(c + 1) * 128, :], ysl[i][:, NHALO:NM], reads=[yslk[i]], store=True)
                barrier(drain=False)
            cstep(fin)

        if maxsteps is not None:
            del steps[maxsteps:]
        run_steps()
        S.emit()
        nops = {e: len(S.ops[e]) for e in S.ENG}
    return nc, nops


def _t5_bucket(dist):
    max_exact = N_BUCKETS // 2
    d = np.maximum(dist, 0)
    df = np.maximum(d, 1).astype(np.float32)
    large = max_exact + (np.log(df / np.float32(max_exact)) / np.float32(math.log(MAX_DISTANCE / max_exact))
                         * np.float32(N_BUCKETS - max_exact)).astype(np.int32)
    large = np.minimum(large, N_BUCKETS - 1)
    return np.where(d < max_exact, d, large)


def fm_vec(v):
    return np.ascontiguousarray(v.reshape(-1, 128).T)


def make_inputs(cfg, inp):
    f32 = np.float32
    D, H, KV, FF = cfg.D, cfg.H, cfg.KV, cfg.FF
    x_prompt = np.asarray(inp["x_prompt"], f32)
    x_sample = np.asarray(inp["x_sample"], f32)[:, 0, :]
    sk = np.asarray(inp["state_k_window"], f32)[0].reshape(DEC_BATCH, 128, cfg.KW)
    sv = np.asarray(inp["state_v_window"], f32)[0].reshape(DEC_BATCH, 128, cfg.KW)
    sc = np.asarray(inp["state_conv"], f32)[0]
    sf = np.asarray(inp["state_ffn_conv"], f32)[0]
    rel_bias = np.asarray(inp["rel_bias"], f32)
    sinks = np.asarray(inp["sinks"], f32)[0]

    shared = {
        "w_in": np.asarray(inp["w_in"], f32)[0], "w_a": np.asarray(inp["w_branch_a"], f32)[0],
        "w_b": np.asarray(inp["w_branch_b"], f32)[0], "w_o": np.asarray(inp["w_out"], f32)[0],
        "w_g": np.asarray(inp["w_ffn_gate"], f32)[0], "w_u": np.asarray(inp["w_ffn_up"], f32)[0],
        "w_d": np.asarray(inp["w_ffn_down"], f32)[0],
        "g1T": fm_vec(np.asarray(inp["attn_norm_g"], f32)[0]), "g2T": fm_vec(np.asarray(inp["ffn_norm_g"], f32)[0]),
        "gfT": fm_vec(np.asarray(inp["final_norm_g"], f32)),
        "cwT": np.ascontiguousarray(np.asarray(inp["conv_w"], f32)[0].reshape(3, -1, 128).transpose(2, 1, 0)),
        "fwT": np.ascontiguousarray(np.asarray(inp["ffn_conv_w"], f32)[0].reshape(3, -1, 128).transpose(2, 1, 0)),
        "fbT": fm_vec(np.asarray(inp["ffn_conv_b"], f32)[0]),
        "sinkb": np.ascontiguousarray(np.broadcast_to(sinks[None, :], (128, H))),
    }
    jj = np.arange(128)[:, None]
    ii = np.arange(128)[None, :]
    bt = np.empty((128, H, 2, 128), f32)
    dprev = ii + 128 - jj
    down = ii - jj
    for v, dist in enumerate((dprev, down)):
        valid = (dist >= 0) & (dist <= WINDOW)
        vals = rel_bias[_t5_bucket(dist)]
        vals = np.where(valid[:, :, None], vals, f32(NEG))
        bt[:, :, v, :] = vals.transpose(0, 2, 1)
    shared["biasT"] = bt
    ds = 128 - np.arange(128)
    shared["biasS"] = np.ascontiguousarray(rel_bias[_t5_bucket(ds)])
    shared["biasSn"] = np.ascontiguousarray(np.broadcast_to(rel_bias[_t5_bucket(np.zeros(1, np.int64))], (128, H)))

    in_maps = []
    for c in range(NCORES):
        b, half = c // 2, c % 2
        m = dict(shared)
        xg = np.zeros((NGRP, NH + NM, D), f32)
        for g in range(NGRP):
            s = half * 1024 + g * GP
            if s >= NH:
                xg[g, 0:NH] = x_prompt[b, s - NH:s]
                xg[g, NH + V0:NH + NHALO] = x_prompt[b, s - (NHALO - V0):s]
            xg[g, NH + NHALO:NH + NHALO + GP] = x_prompt[b, s:s + GP]
            xg[g, NH + NHALO + GP:] = x_sample[c * 16 + g * GS:c * 16 + (g + 1) * GS]
        m["xT"] = np.ascontiguousarray(xg.transpose(0, 2, 1))
        m["fmask"] = np.full((128, 1), NEG if half == 0 else 0.0, f32)
        sl = slice(c * 16, c * 16 + 16)
        m["st_k"], m["st_v"] = sk[sl], sv[sl]
        m["st_c"], m["st_f"] = sc[sl], sf[sl]
        m["scT"] = np.ascontiguousarray(sc[sl].reshape(NGRP, GS, 2, -1, 128).transpose(0, 4, 3, 2, 1))
        m["sfT"] = np.ascontiguousarray(sf[sl].reshape(NGRP, GS, 2, -1, 128).transpose(0, 4, 3, 2, 1))
        k4 = sk[sl].reshape(NGRP, GS, 128, KV, HD)
        kt = k4.transpose(0, 1, 4, 3, 2)
        m["kstT"] = np.ascontiguousarray(np.concatenate([kt, kt], axis=2))
        v4 = sv[sl].reshape(NGRP, GS, 128, KV, HD)
        m["vstd"] = np.ascontiguousarray(np.concatenate([v4, v4], axis=4))
        in_maps.append(m)
    return in_maps


def assemble(cfg, res, inp):
    f32 = np.float32
    D, KV, FF = cfg.D, cfg.KV, cfg.FF
    y_prompt = np.empty((BATCH, SEQ, D), f32)
    y_sample = np.empty((DEC_BATCH, 1, D), f32)
    kp = np.empty((1, BATCH, 128, KV, HD), f32)
    vp = np.empty((1, BATCH, 128, KV, HD), f32)
    cp = np.empty((1, BATCH, 2, cfg.CW), f32)
    fp = np.empty((1, BATCH, 2, FF), f32)
    ksn = np.empty((1, DEC_BATCH, 128, KV, HD), f32)
    vsn = np.empty((1, DEC_BATCH, 128, KV, HD), f32)
    cs = np.empty((1, DEC_BATCH, 2, cfg.CW), f32)
    fs = np.empty((1, DEC_BATCH, 2, FF), f32)
    for c in range(NCORES):
        r = res[c]
        b, half = c // 2, c % 2
        for g in range(NGRP):
            s = half * 1024 + g * GP
            yt = r["yT"][g]
            y_prompt[b, s:s + GP] = yt[:, :GP].T
            y_sample[c * 16 + g * GS:c * 16 + (g + 1) * GS, 0] = yt[:, GP:].T
            sl = slice(c * 16 + g * GS, c * 16 + (g + 1) * GS)
            uc = r["ucT_o"][g].transpose(2, 1, 0).reshape(2 + GS, -1)
            gc = r["gcT_o"][g].transpose(2, 1, 0).reshape(2 + GS, -1)
            cs[0, sl, 1] = uc[2:]
            fs[0, sl, 1] = gc[2:]
            ksn[0, sl, 127] = r["ks_new"][g].reshape(GS, KV, HD)
            vsn[0, sl, 127] = r["vs_new"][g].reshape(GS, KV, HD)
            if half == 1 and g == NGRP - 1:
                kp[0, b] = r["kp_o"][g].reshape(128, KV, HD)
                vp[0, b] = r["vp_o"][g].reshape(128, KV, HD)
                cp[0, b] = uc[:2]
                fp[0, b] = gc[:2]
        sl = slice(c * 16, c * 16 + 16)
        ksn[0, sl, :127] = r["ks_o"].reshape(16, 127, KV, HD)
        vsn[0, sl, :127] = r["vs_o"].reshape(16, 127, KV, HD)
        cs[0, sl, 0] = r["c0_o"]
        fs[0, sl, 0] = r["f0_o"]
    return (y_prompt, y_sample, kp, vp, cp, fp, ksn, vsn, cs, fs)


def kernel(**inputs):
    cfg = REAL
    nc, _ = build(cfg)
    in_maps = make_inputs(cfg, inputs)
    res = run_bass_kernel_spmd(nc, in_maps, core_ids=list(range(NCORES)))
    return assemble(cfg, res.results, inputs)
```
